# Optimizing a Trainium2 kernel written in Bass

```python
import math
import jax, jax.numpy as jnp
from jax import lax
import numpy as np

D_MODEL = 1024
BATCH = 1
SEQ = 16384
DEPTH = 1

D_MIX = D_MODEL
D_HGRN = D_MIX // 2
D_DIFF = D_MIX - D_HGRN
HGRN_EXPAND = 128
HGRN_HEADS = D_HGRN // HGRN_EXPAND
HGRN_DK = HGRN_EXPAND
HGRN_DV = D_HGRN // HGRN_HEADS
HGRN_CHUNK = 64
DIFF_HEADS = 4
DIFF_VDIM = D_DIFF // DIFF_HEADS
DIFF_QKDIM = DIFF_VDIM // 2
Q_BLOCK = 128
ROPE_THETA = 10000.0
NORM_EPS = 1e-6
SUBLN_EPS = 1e-5
LAMBDA_STD = 0.1
SPLIT_WIDTHS = (D_HGRN, D_HGRN, D_HGRN, D_HGRN,
                D_DIFF, D_DIFF, D_DIFF, D_DIFF)
D_IN = sum(SPLIT_WIDTHS)
SPLIT_POINTS = tuple(int(v) for v in np.cumsum(SPLIT_WIDTHS)[:-1])

kernel_name = "hymba_style_hgrn2_diffattn_hybrid"


def rmsnorm(x, w, eps=NORM_EPS):
    xf = x.astype(jnp.float32)
    y = xf * lax.rsqrt(jnp.mean(xf * xf, axis=-1, keepdims=True) + eps)
    return (y * w.astype(jnp.float32)).astype(x.dtype)


def rope_tables(T):
    half = DIFF_QKDIM // 2
    inv_freq = 1.0 / (ROPE_THETA ** (jnp.arange(half, dtype=jnp.float32) / half))
    ang = jnp.arange(T, dtype=jnp.float32)[:, None] * inv_freq[None, :]
    return jnp.cos(ang), jnp.sin(ang)


def apply_rope(t, cos, sin):
    c = cos[:, None, None, :].astype(t.dtype)
    s = sin[:, None, None, :].astype(t.dtype)
    t1, t2 = jnp.split(t, 2, axis=-1)
    return jnp.concatenate([t1 * c - t2 * s, t2 * c + t1 * s], axis=-1)


def hgrn2_mixer(q, f_pre, i, lb):
    B, T, _ = q.shape
    N = T // HGRN_CHUNK
    f = lb + (1.0 - lb) * jax.nn.sigmoid(f_pre.astype(jnp.float32))
    k = 1.0 - f
    g = jnp.log(f)

    def heads(t, d):
        return t.astype(jnp.float32).reshape(B, N, HGRN_CHUNK, HGRN_HEADS, d).transpose(1, 0, 3, 2, 4)

    qc, kc, gc = heads(q, HGRN_DK), heads(k, HGRN_DK), heads(g, HGRN_DK)
    vc = heads(i, HGRN_DV)
    bc = jnp.cumsum(gc, axis=3)
    causal = jnp.tril(jnp.ones((HGRN_CHUNK, HGRN_CHUNK), dtype=bool))[None, None, :, :, None]

    def step(S, inp):
        q_, k_, v_, b_ = inp
        inter = jnp.einsum('bhcd,bhde->bhce', q_ * jnp.exp(b_), S)
        diff = b_[:, :, :, None, :] - b_[:, :, None, :, :]
        decay = jnp.where(causal, jnp.exp(jnp.where(causal, diff, 0.0)), 0.0)
        A = jnp.einsum('bhtd,bhsd,bhtsd->bhts', q_, k_, decay)
        intra = jnp.einsum('bhts,bhse->bhte', A, v_)
        b_last = b_[:, :, -1, :]
        S_new = jnp.exp(b_last)[..., None] * S + jnp.einsum(
            'bhsd,bhse->bhde', k_ * jnp.exp(b_last[:, :, None, :] - b_), v_)
        return S_new, inter + intra

    S0 = jnp.zeros((B, HGRN_HEADS, HGRN_DK, HGRN_DV), jnp.float32)
    _, ys = lax.scan(step, S0, (qc, kc, vc, bc))
    return ys.transpose(1, 0, 3, 2, 4).reshape(B, T, HGRN_HEADS, HGRN_DV)


def diff_attention(q, k, v, lam):
    B, H, _, T, dh = q.shape
    NB = T // Q_BLOCK
    scale = dh ** -0.5
    q_blocks = q.reshape(B, H, 2, NB, Q_BLOCK, dh).transpose(3, 0, 1, 2, 4, 5)
    q_pos = jnp.arange(T).reshape(NB, Q_BLOCK)
    k_pos = jnp.arange(T)

    def attend(args):
        qb, qp = args
        s = jnp.einsum('bhcqd,bhckd->bhcqk', qb, k).astype(jnp.float32) * scale
        s = jnp.where((qp[:, None] >= k_pos[None, :])[None, None, None], s, -jnp.inf)
        p = jax.nn.softmax(s, axis=-1)
        w = p[:, :, 0] - lam * p[:, :, 1]
        return jnp.einsum('bhqk,bhke->bhqe', w.astype(v.dtype), v)

    o = lax.map(attend, (q_blocks, q_pos))
    return o.transpose(1, 0, 3, 2, 4).reshape(B, T, H, DIFF_VDIM)


def setup_inputs(seed: int = 0) -> dict:
    key = jax.random.key(seed)
    ks = jax.random.split(key, 12)
    f32 = jnp.float32
    x = jax.random.normal(ks[0], (BATCH, SEQ, D_MODEL), f32)
    norm_w = 1.0 + 0.02 * jax.random.normal(ks[1], (DEPTH, D_MODEL), f32)
    w_in = jax.random.normal(ks[2], (DEPTH, D_MODEL, D_IN), f32) * D_MODEL ** -0.5
    hgrn_lb_logits = 0.1 * jax.random.normal(ks[3], (DEPTH + 1, D_HGRN), f32)
    hgrn_norm_w = 1.0 + 0.02 * jax.random.normal(ks[4], (DEPTH, HGRN_DV), f32)
    diff_lambda_q1 = LAMBDA_STD * jax.random.normal(ks[5], (DEPTH, DIFF_QKDIM), f32)
    diff_lambda_k1 = LAMBDA_STD * jax.random.normal(ks[6], (DEPTH, DIFF_QKDIM), f32)
    diff_lambda_q2 = LAMBDA_STD * jax.random.normal(ks[7], (DEPTH, DIFF_QKDIM), f32)
    diff_lambda_k2 = LAMBDA_STD * jax.random.normal(ks[8], (DEPTH, DIFF_QKDIM), f32)
    diff_norm_w = 1.0 + 0.02 * jax.random.normal(ks[9], (DEPTH, DIFF_VDIM), f32)
    w_out = jax.random.normal(ks[10], (DEPTH, D_MIX, D_MODEL), f32) * (D_MIX * 2 * DEPTH) ** -0.5
    final_norm_w = 1.0 + 0.02 * jax.random.normal(ks[11], (D_MODEL,), f32)
    return {"x": x, "norm_w": norm_w, "w_in": w_in, "hgrn_lb_logits": hgrn_lb_logits,
            "hgrn_norm_w": hgrn_norm_w, "diff_lambda_q1": diff_lambda_q1,
            "diff_lambda_k1": diff_lambda_k1, "diff_lambda_q2": diff_lambda_q2,
            "diff_lambda_k2": diff_lambda_k2, "diff_norm_w": diff_norm_w,
            "w_out": w_out, "final_norm_w": final_norm_w}


def reference(x, norm_w, w_in, hgrn_lb_logits, hgrn_norm_w, diff_lambda_q1, diff_lambda_k1,
              diff_lambda_q2, diff_lambda_k2, diff_norm_w, w_out, final_norm_w):
    B, T, _ = x.shape
    cos, sin = rope_tables(T)
    lb_all = jnp.cumsum(jax.nn.softmax(hgrn_lb_logits.astype(jnp.float32), axis=0), axis=0)
    for l in range(DEPTH):
        h = rmsnorm(x, norm_w[l])
        proj = h @ w_in[l].astype(h.dtype)
        hq, hf, hi, hg, dq, dk, dv, dg = jnp.split(proj, SPLIT_POINTS, axis=-1)

        ho = hgrn2_mixer(hq, hf, hi, lb_all[l])
        ho = rmsnorm(ho, hgrn_norm_w[l]).reshape(B, T, D_HGRN)
        ho = ho.astype(x.dtype) * jax.nn.silu(hg)

        qd = apply_rope(dq.reshape(B, T, DIFF_HEADS, 2, DIFF_QKDIM), cos, sin).transpose(0, 2, 3, 1, 4)
        kd = apply_rope(dk.reshape(B, T, DIFF_HEADS, 2, DIFF_QKDIM), cos, sin).transpose(0, 2, 3, 1, 4)
        vd = dv.reshape(B, T, DIFF_HEADS, DIFF_VDIM).transpose(0, 2, 1, 3)
        lambda_init = 0.8 - 0.6 * math.exp(-0.3 * l)
        lam = (jnp.exp(jnp.sum(diff_lambda_q1[l].astype(jnp.float32) * diff_lambda_k1[l].astype(jnp.float32)))
               - jnp.exp(jnp.sum(diff_lambda_q2[l].astype(jnp.float32) * diff_lambda_k2[l].astype(jnp.float32)))
               + lambda_init)
        do = diff_attention(qd, kd, vd, lam)
        do = rmsnorm(do, diff_norm_w[l], SUBLN_EPS) * (1.0 - lambda_init)
        do = do.reshape(B, T, D_DIFF).astype(x.dtype) * jax.nn.silu(dg)

        mix = jnp.concatenate([ho, do], axis=-1)
        x = x + mix @ w_out[l].astype(mix.dtype)
    return rmsnorm(x, final_norm_w)
```

```python
import numpy as np
import ml_dtypes
import concourse.bass as bass
import concourse.mybir as mybir
from concourse.bass_utils import run_bass_kernel_spmd

F32 = mybir.dt.float32
BF16 = mybir.dt.bfloat16
AF = mybir.ActivationFunctionType
ALU = mybir.AluOpType
AX = mybir.AxisListType
ESZ = {F32: 4, BF16: 2}
GRAN = 64

NCORES = 8
T = 16384
D = 1024
TPC = 2048
NT = 16
LAMBDA_INIT = 0.8 - 0.6 * 1.0
NEG = -30000.0


class View:
    __slots__ = ("ap", "space", "regs")

    def __init__(self, ap, space, regs):
        self.ap, self.space, self.regs = ap, space, regs


class Arena:
    def __init__(self, base_ap_f32, space, cap):
        self.base, self.space, self.cap = base_ap_f32, space, cap
        self.free = [(0, cap)]
        self.peak = 0

    def alloc(self, nbytes):
        nbytes = (nbytes + GRAN - 1) // GRAN * GRAN
        for i, (o, n) in enumerate(self.free):
            if n >= nbytes:
                if n == nbytes:
                    self.free.pop(i)
                else:
                    self.free[i] = (o + nbytes, n - nbytes)
                self.peak = max(self.peak, o + nbytes)
                return o, nbytes
        raise MemoryError(f"{self.space} arena full: want {nbytes}, free={self.free}")

    def release(self, off, nbytes):
        self.free.append((off, nbytes))
        self.free.sort()
        m = []
        for o, n in self.free:
            if m and m[-1][0] + m[-1][1] == o:
                m[-1] = (m[-1][0], m[-1][1] + n)
            else:
                m.append((o, n))
        self.free = m


class Buf:
    def __init__(self, arena, dtype, n):
        self.arena, self.dtype, self.n = arena, dtype, n
        self.esz = ESZ[dtype]
        self.off, self.nbytes = arena.alloc(n * self.esz)
        a = arena.base[:, self.off // 4:(self.off + self.nbytes) // 4]
        if dtype != F32:
            a = a.bitcast(dtype)
        self.ap = a[:, 0:n]
        self.space = arena.space

    def free(self):
        self.arena.release(self.off, self.nbytes)

    def v(self, a=0, b=None, p0=0, p1=128):
        if b is None:
            b = self.n
        return View(self.ap[p0:p1, a:b], self.space,
                    [(self.off + a * self.esz, self.off + b * self.esz)])

    def cv(self, ap, ranges=None):
        if ranges is None:
            ranges = [(0, self.n)]
        return View(ap, self.space,
                    [(self.off + a * self.esz, self.off + b * self.esz) for a, b in ranges])


class DT:
    def __init__(self, nc, name, shape, dtype, kind=None):
        if kind is None:
            self.t = nc.dram_tensor(name, shape, dtype)
        else:
            self.t = nc.dram_tensor(name, shape, dtype, kind=kind)
        self.ap = self.t.ap()
        self.space = "dram_" + name

    def v(self, ap=None):
        return View(self.ap if ap is None else ap, self.space, [(0, GRAN)])


class Op:
    __slots__ = ("eng", "fn", "reads", "writes", "kind", "group", "idx", "deps", "signal", "inc")


class Sched:
    ENGS = ("pe", "act", "dve", "pool", "sp")

    def __init__(self):
        self.ops = []
        self.groups = {}

    def add(self, eng, fn, reads=(), writes=(), kind="c", group=None, inc=16):
        op = Op()
        op.eng, op.fn, op.reads, op.writes = eng, fn, list(reads), list(writes)
        op.kind, op.group, op.idx = kind, group, len(self.ops)
        op.deps, op.signal, op.inc = {}, False, inc
        if kind != "c":
            self.groups.setdefault(group, []).append(op.idx)
        self.ops.append(op)
        return op

    def dma(self, eng, out, in_, group, **kw):
        r = [in_] if isinstance(in_, View) else []
        w = [out] if isinstance(out, View) else []
        oa = out.ap if isinstance(out, View) else out
        ia = in_.ap if isinstance(in_, View) else in_
        return self.add(eng, lambda e: e.dma_start(out=oa, in_=ia, **kw), r, w, kind="d", group=group)

    def analyze(self):
        lastw, readers = {}, {}
        ops = self.ops
        for op in ops:
            deps = {}
            rblocks, wblocks = [], []
            for v in op.reads:
                g_ = 2048 if v.space == "ps" else GRAN
                for lo, hi in v.regs:
                    for blk in range(lo // g_, (hi - 1) // g_ + 1):
                        rblocks.append((v.space, blk))
                        if v.space == "ps":
                            wblocks.append((v.space, blk))
            for v in op.writes:
                g_ = 2048 if v.space == "ps" else GRAN
                for lo, hi in v.regs:
                    for blk in range(lo // g_, (hi - 1) // g_ + 1):
                        wblocks.append((v.space, blk))
            for key in rblocks:
                w = lastw.get(key)
                if w is not None:
                    deps[w] = True
            for key in wblocks:
                w = lastw.get(key)
                if w is not None and w not in deps:
                    deps[w] = False
                rl = readers.get(key)
                if rl:
                    for r in rl:
                        if r not in deps:
                            deps[r] = False
            for key in rblocks:
                rl = readers.get(key)
                if not rl:
                    readers[key] = [op.idx]
                elif rl[-1] != op.idx:
                    rl.append(op.idx)
            for key in wblocks:
                lastw[key] = op.idx
                readers[key] = []
            deps.pop(op.idx, None)
            chan = {}
            for p, raw in deps.items():
                po = ops[p]
                if po.kind == "c":
                    if po.eng == op.eng and op.kind == "c":
                        if po.eng == "pe" or not raw:
                            continue
                    key = ("e", po.eng)
                else:
                    key = ("g", po.group)
                if key not in chan or chan[key] < p:
                    chan[key] = p
            op.deps = chan
            for p in chan.values():
                ops[p].signal = True
        tickets = {}
        cnt = {e: 0 for e in self.ENGS}
        for op in ops:
            if op.kind == "c" and op.signal:
                cnt[op.eng] += 1
                tickets[op.idx] = cnt[op.eng]
        self.tickets, self.counts = tickets, cnt
        self.gpos = {}
        for g, lst in self.groups.items():
            acc, arr = 0, []
            for i in lst:
                acc += ops[i].inc
                arr.append((i, acc))
            self.gpos[g] = arr

    def gcount_before(self, g, idx):
        v = 0
        for i, acc in self.gpos[g]:
            if i < idx:
                v = acc
            else:
                break
        return v

    def emit_engine(self, engname, h, esem, gsem):
        waited = {}
        ops = self.ops
        for op in ops:
            if op.eng != engname:
                continue
            for key, p in op.deps.items():
                if key[0] == "e":
                    sem, val = esem[key[1]], self.tickets[p]
                else:
                    sem, val = gsem[key[1]], self.gcount_before(key[1], op.idx)
                if waited.get(key, 0) >= val:
                    continue
                waited[key] = val
                h.wait_ge(sem, val)
            ins = op.fn(h)
            if op.kind == "c":
                if op.signal:
                    ins.then_inc(esem[engname], 1)
            else:
                ins.then_inc(gsem[op.group], op.inc)


def build_program(dbg=()):
    nc = bass.Bass("TRN2", target_bir_lowering=False)
    S = Sched()
    PID = {}

    def din(name, shape, dt):
        return nc.dram_tensor(name, shape, dt, kind="ExternalInput").ap()

    xc = din("xc", [TPC, D], F32)
    xi = din("xi", [TPC, D], F32)
    w_in = din("w_in", [D, 4096], F32)
    w_out = din("w_out", [D, D], F32)
    nw_d = din("nw", [128, 8], F32)
    lbl_d = din("lbl", [128, 8], F32)
    hnw_d = din("hnw", [128, 1], F32)
    dnw_d = din("dnw", [128, 1], F32)
    lamv_d = din("lamv", [128, 256], F32)
    fnw_d = din("fnw", [128, D], F32)
    ropek_d = din("ropek", [128, 2 * TPC], F32)
    ropeq_d = din("ropeq", [128, 2 * TPC], F32)
    amask_d = din("amask", [128, 8 * 128], BF16)
    hmask_d = din("hmask", [128, 128], F32)
    ident_d = din("ident", [128, 128], BF16)
    sel_d = din("sel", [128, 8], F32)
    y_d = nc.dram_tensor("y", [TPC, D], F32, kind="ExternalOutput").ap()
    dbg_out = {}

    kb = [DT(nc, f"kb{h}", [128, TPC], BF16) for h in range(4)]
    kg1 = [DT(nc, f"kg1{h}", [4 * 128, TPC], BF16) for h in range(4)]
    kg = [DT(nc, f"kg{h}", [NCORES * 128, TPC], BF16) for h in range(4)]
    vb = [DT(nc, f"vb{h}", [256, 1024], BF16) for h in range(4)]
    vg1 = [DT(nc, f"vg1{h}", [4 * 256, 1024], BF16) for h in range(4)]
    vg = [DT(nc, f"vg{h}", [NCORES * 256, 1024], BF16) for h in range(4)]
    stb = [DT(nc, f"stb{h}", [128, 129], F32) for h in range(4)]
    stg1 = [DT(nc, f"stg1{h}", [4 * 128, 129], F32) for h in range(4)]
    stg = [DT(nc, f"stg{h}", [NCORES * 128, 129], F32) for h in range(4)]
    hob = [DT(nc, f"hob{h}", [128, TPC], BF16) for h in range(4)]
    hog1 = [DT(nc, f"hog1{h}", [4 * 128, TPC], BF16) for h in range(4)]
    hog = [DT(nc, f"hog{h}", [NCORES * 128, TPC], BF16) for h in range(4)]
    RG_DIE = [[0, 1, 2, 3], [4, 5, 6, 7]]
    RG_PAIR = [[0, 4], [1, 5], [2, 6], [3, 7]]

    def allgather2(src, mid, dst, name):
        S.add("pool", lambda e: e.collective_compute("AllGather", ALU.bypass, replica_groups=RG_DIE,
                                                     ins=[src.ap.opt()], outs=[mid.ap.opt()]),
              [src.v()], [mid.v()], kind="cc", group="cc1" + name, inc=1)
        S.add("pool", lambda e: e.collective_compute("AllGather", ALU.bypass, replica_groups=RG_PAIR,
                                                     ins=[mid.ap.opt()], outs=[dst.ap.opt()]),
              [mid.v()], [dst.v()], kind="cc", group="cc2" + name, inc=1)

    SB_CAP = 206 * 1024
    sb_t = nc.alloc_sbuf_tensor("sbuf_all", [128, SB_CAP // 4], F32)
    ps_t = nc.alloc_psum_tensor("psum_all", [128, 4096], F32)
    SB = Arena(sb_t.ap(), "sb", SB_CAP)
    PS = Arena(ps_t.ap(), "ps", 16384)

    def sbuf(dt, n):
        return Buf(SB, dt, n)

    def bank(dt=F32):
        return Buf(PS, dt, 2048 // ESZ[dt])

    def act(o, i, func, scale=1.0, bias=None, accum=None, eng="act"):
        kw = {}
        if bias is not None:
            kw["bias"] = bias.ap if isinstance(bias, View) else bias
        if accum is not None:
            kw["accum_out"] = accum.ap
        sc = scale.ap if isinstance(scale, View) else scale
        r = [i] + [x for x in (bias, scale) if isinstance(x, View)]
        w = [o] + ([accum] if accum is not None else [])
        S.add("act", lambda e: e.activation(o.ap, i.ap, func, scale=sc, **kw), r, w)

    def ts(eng, o, i, s1, s2, op0, op1=None):
        a1 = s1.ap if isinstance(s1, View) else s1
        a2 = s2.ap if isinstance(s2, View) else s2
        r = [i] + [x for x in (s1, s2) if isinstance(x, View)]
        if op1 is None:
            S.add(eng, lambda e: e.tensor_scalar(o.ap, i.ap, a1, None, op0), r, [o])
        else:
            S.add(eng, lambda e: e.tensor_scalar(o.ap, i.ap, a1, a2, op0, op1), r, [o])

    def tt(eng, o, a, b, op):
        S.add(eng, lambda e: e.tensor_tensor(o.ap, a.ap, b.ap, op), [a, b], [o])

    def stt(eng, o, a, s, b, op0, op1):
        sa = s.ap if isinstance(s, View) else s
        r = [a, b] + ([s] if isinstance(s, View) else [])
        S.add(eng, lambda e: e.scalar_tensor_tensor(o.ap, a.ap, sa, b.ap, op0, op1), r, [o])

    def cp(eng, o, i):
        if eng == "act":
            S.add("act", lambda e: e.activation(o.ap, i.ap, AF.Copy), [i], [o])
        else:
            S.add(eng, lambda e: e.tensor_copy(o.ap, i.ap), [i], [o])

    def mm(o, l, r, start=True, stop=True):
        S.add("pe", lambda e: e.matmul(o.ap, l.ap, r.ap, start=start, stop=stop), [l, r], [o])

    def tr(o, i, idv):
        S.add("pe", lambda e: e.transpose(o.ap, i.ap, idv.ap), [i, idv], [o])

    def dump(name, buf, dt, n):
        d = nc.dram_tensor("dbg_" + name, [128, n], dt, kind="ExternalOutput").ap()
        dbg_out[name] = True
        S.dma("sp", d, buf.v(0, n), "dbg")

    ident = sbuf(BF16, 128)
    ones_bf = sbuf(BF16, 128)
    ones_f = sbuf(F32, 128)
    hmask = sbuf(F32, 128)
    amask = sbuf(BF16, 1024)
    rmask = sbuf(F32, 1024)
    nwt = sbuf(F32, 16)
    lbl = sbuf(F32, 16)
    lb = sbuf(F32, 16)
    oml = sbuf(F32, 16)
    hnw = sbuf(F32, 16)
    dnw = sbuf(F32, 16)
    sel = sbuf(F32, 16)
    lamv = sbuf(F32, 256)
    lamt = sbuf(F32, 128)
    ls = sbuf(F32, 16)
    neglam = sbuf(F32, 16)
    fnw = sbuf(F32, D)
    for dst, src, n in ((ident, ident_d, 128), (hmask, hmask_d, 128), (amask, amask_d, 1024),
                        (nwt, nw_d, 8), (lbl, lbl_d, 8), (hnw, hnw_d, 1), (dnw, dnw_d, 1),
                        (sel, sel_d, 8), (lamv, lamv_d, 256), (fnw, fnw_d, D)):
        S.dma("sp", dst.v(0, n), src, "const")
    S.add("pool", lambda e: e.memset(ones_bf.ap, 1.0), [], [ones_bf.v()])
    S.add("pool", lambda e: e.memset(ones_f.ap, 1.0), [], [ones_f.v()])
    S.add("pool", lambda e: e.memset(rmask.ap, 1.0), [], [rmask.v()])
    S.add("pool", lambda e: e.memset(rmask.ap.rearrange("p (c t) -> p c t", t=64)[:, :, 0:1], 0.0),
          [], [rmask.v()])
    tt("dve", lb.v(0, 4), lbl.v(4, 8), lbl.v(0, 4), ALU.subtract)
    act(lb.v(0, 4), lb.v(0, 4), AF.Exp)
    ts("dve", lb.v(0, 4), lb.v(0, 4), 1.0, None, ALU.add)
    S.add("dve", lambda e: e.reciprocal(lb.ap[:, 0:4], lb.ap[:, 0:4]), [lb.v(0, 4)], [lb.v(0, 4)])
    ts("dve", oml.v(0, 4), lb.v(0, 4), -1.0, 1.0, ALU.mult, ALU.add)
    ts("dve", dnw.v(0, 1), dnw.v(0, 1), 1.0 - LAMBDA_INIT, None, ALU.mult)
    tt("dve", lamt.v(0, 64), lamv.v(0, 64), lamv.v(64, 128), ALU.mult)
    tt("dve", lamt.v(64, 128), lamv.v(128, 192), lamv.v(192, 256), ALU.mult)
    S.add("dve", lambda e: e.reduce_sum(ls.ap[:, 0:1], lamt.ap[:, 0:64], AX.X), [lamt.v(0, 64)], [ls.v(0, 1)])
    S.add("dve", lambda e: e.reduce_sum(ls.ap[:, 1:2], lamt.ap[:, 64:128], AX.X), [lamt.v(64, 128)], [ls.v(1, 2)])
    act(ls.v(0, 2), ls.v(0, 2), AF.Exp)
    tt("dve", neglam.v(0, 1), ls.v(1, 2), ls.v(0, 1), ALU.subtract)
    ts("dve", neglam.v(0, 1), neglam.v(0, 1), -LAMBDA_INIT, None, ALU.add)

    wst = [sbuf(F32, 1024) for _ in range(2)]
    wst_i = [0]

    def load_w_cast(dst, dcol, src_ap, n, kc, swap_to=None):
        i = wst_i[0] % 2
        wst_i[0] += 1
        st = wst[i]
        if isinstance(src_ap, list):
            for j, sa in enumerate(src_ap):
                S.dma("sp", st.v(j * 512, (j + 1) * 512), sa, f"wst{i}")
        else:
            S.dma("sp", st.v(0, n), src_ap, f"wst{i}")
        return st

    wsc_i = [0]

    def wscale(dst_view, src_view, kc):
        wsc_i[0] += 1
        if wsc_i[0] % 2:
            act(dst_view, src_view, AF.Copy, scale=nwt.v(kc, kc + 1))
        else:
            ts("dve", dst_view, src_view, nwt.v(kc, kc + 1), None, ALU.mult)

    hT = sbuf(BF16, 8 * TPC)
    hT3 = hT.ap.rearrange("p (k t) -> p k t", k=8)
    PH = {}

    def alloc_AD():
        PH["xs"] = [sbuf(F32, D) for _ in range(3)]
        PH["hsb"] = [sbuf(BF16, D) for _ in range(2)]
        PH["junk"] = sbuf(BF16, D)
        PH["wA"] = sbuf(BF16, 8 * 1536)
        PH["rope"] = sbuf(F32, 2 * TPC)
        PH["qkT"] = sbuf(BF16, 4 * TPC)

    def free_AD(keep=()):
        for k_ in ("xs", "hsb", "junk", "wA", "rope", "qkT"):
            if k_ in keep:
                continue
            v_ = PH.pop(k_)
            for b_ in (v_ if isinstance(v_, list) else [v_]):
                b_.free()

    alloc_AD()
    stat = [sbuf(F32, 16) for _ in range(4)]
    pT = [bank(BF16) for _ in range(2)]

    def nt_stage1(x_dram, n, cnt):
        xb = PH["xs"][cnt % 3]
        hb_ = PH["hsb"][cnt % 2]
        junk = PH["junk"]
        st = stat[cnt % 4]
        S.dma("sp", xb.v(), x_dram[n * 128:(n + 1) * 128, :], f"xs{cnt % 3}")
        act(junk.v(), xb.v(), AF.Square, accum=st.v(0, 1))
        act(st.v(2, 3), st.v(0, 1), AF.Ln, scale=1.0 / D, bias=1e-6)
        act(st.v(3, 4), st.v(2, 3), AF.Exp, scale=-0.5)
        ts("dve", hb_.v(), xb.v(), st.v(3, 4), None, ALU.mult)

    def nt_stage2(n, cnt):
        hb_ = PH["hsb"][cnt % 2]
        pt = pT[cnt % len(pT)]
        for k in range(8):
            tr(pt.v(k * 128, (k + 1) * 128), hb_.v(k * 128, (k + 1) * 128), ident.v())
        dstv = hT.cv(hT3[:, :, n * 128:(n + 1) * 128],
                     [(k * TPC + n * 128, k * TPC + (n + 1) * 128) for k in range(8)])
        srcv = pt.cv(pt.ap.rearrange("p (k t) -> p k t", k=8))
        cp("act" if cnt % 2 else "dve", dstv, srcv)

    def norm_transpose_all(x_dram, cnt0, hook=None):
        nt_stage1(x_dram, 0, cnt0)
        for n in range(NT):
            if n + 1 < NT:
                nt_stage1(x_dram, n + 1, cnt0 + n + 1)
            if hook is not None:
                hook(n)
            nt_stage2(n, cnt0 + n)
        return cnt0 + NT

    def hTv(kc, a, b):
        return hT.v(kc * TPC + a, kc * TPC + b)

    def wAv(kc, a, b):
        return PH["wA"].v(kc * 1536 + a, kc * 1536 + b)

    def cast_group_qk(wbuf, wv, kc, st):
        wscale(wv(kc, 0, 512), st.v(0, 512), kc)
        wscale(wv(kc, 1024, 1536), st.v(512, 1024), kc)
        s4 = st.ap[:, 0:512].rearrange("p (g h d) -> p g h d", g=8, h=2)
        d4 = wbuf.ap[:, kc * 1536 + 512:kc * 1536 + 1024].rearrange("p (g h d) -> p g h d", g=8, h=2)
        for hh in range(2):
            wscale(wbuf.cv(d4[:, :, hh, :], [(kc * 1536 + 512, kc * 1536 + 1024)]),
                   st.cv(s4[:, :, 1 - hh, :], [(0, 512)]), kc)

    for kc in range(8):
        st = load_w_cast(None, 0, w_in[kc * 128:(kc + 1) * 128, 2560:3584], 1024, kc)
        cast_group_qk(PH['wA'], wAv, kc, st)

    S.dma("sp", PH["rope"].v(), ropek_d, "rope")

    wB = sbuf(BF16, 8 * 2048)

    def wBv(kc, a, b):
        return wB.v(kc * 2048 + a, kc * 2048 + b)

    def wB_hook(n):
        kc, half = n // 2, n % 2
        st = load_w_cast(None, 0, w_in[kc * 128:(kc + 1) * 128, half * 1024:(half + 1) * 1024], 1024, kc)
        wscale(wBv(kc, half * 1024, (half + 1) * 1024), st.v(), kc)

    cnt = norm_transpose_all(xc, 0, hook=wB_hook)

    if "hT" in dbg:
        dump("hT", hT, BF16, 8 * TPC)

    pj = [bank() for _ in range(4)]
    pj_i = [0]

    def next_pj():
        b_ = pj[pj_i[0] % len(pj)]
        pj_i[0] += 1
        return b_

    tmp1 = [sbuf(F32, 512) for _ in range(2)]
    tmp2 = [sbuf(F32, 512) for _ in range(2)]

    def proj_fm(ps, wv, kc_cols, tg):
        a, b = kc_cols
        for kc in range(8):
            mm(ps.v(0, 512), wv(kc, a, b), hTv(kc, tg * 512, (tg + 1) * 512), start=(kc == 0), stop=(kc == 7))

    def rope_proj(wv, dst, i):
        rope = PH["rope"]
        for h in range(4):
            for tg in range(4):
                p1, p2 = next_pj(), next_pj()
                proj_fm(p1, wv, (h * 128, (h + 1) * 128), tg)
                proj_fm(p2, wv, (512 + h * 128, 512 + (h + 1) * 128), tg)
                t1, t2 = tmp1[i[0] % 2], tmp2[i[0] % 2]
                i[0] += 1
                tt("dve", t1.v(), p1.v(0, 512), rope.v(tg * 512, (tg + 1) * 512), ALU.mult)
                tt("dve", t2.v(), p2.v(0, 512), rope.v(TPC + tg * 512, TPC + (tg + 1) * 512), ALU.mult)
                tt("dve", dst.v(h * TPC + tg * 512, h * TPC + (tg + 1) * 512), t1.v(), t2.v(), ALU.add)

    ri = [0]
    qkT = PH['qkT']
    rope_proj(wAv, qkT, ri)
    for h in range(4):
        S.dma("pool", kb[h].v(), qkT.v(h * TPC, (h + 1) * TPC), "kb")
    vfull = sbuf(BF16, NT * 512)
    vst = []
    hv = sbuf(BF16, NT * 512)
    for n in range(NT):
        ps, ps2 = next_pj(), next_pj()
        for kc in range(8):
            mm(ps.v(0, 512), hTv(kc, n * 128, (n + 1) * 128), wAv(kc, 1024, 1536), start=(kc == 0), stop=(kc == 7))
            mm(ps2.v(0, 512), hTv(kc, n * 128, (n + 1) * 128), wBv(kc, 1024, 1536), start=(kc == 0), stop=(kc == 7))
        cp("act", vfull.v(n * 512, (n + 1) * 512), ps.v(0, 512))
        cp("dve", hv.v(n * 512, (n + 1) * 512), ps2.v(0, 512))
        if n % 8 == 7:
            s_ = n // 8
            vf4 = vfull.ap.rearrange("p (n h e) -> p n h e", n=NT, h=4)
            for h in range(4):
                dst = vb[h].ap[s_ * 128:(s_ + 1) * 128, :].rearrange("p (n e) -> p n e", n=8)
                S.dma("sp", vb[h].v(dst),
                      vfull.cv(vf4[:, s_ * 8:(s_ + 1) * 8, h, :], [(s_ * 8 * 512, (s_ + 1) * 8 * 512)]), "vb")
    def gather_kv0():
        allgather2(kb[0], kg1[0], kg[0], "K0")
        allgather2(vb[0], vg1[0], vg[0], "V0")

    gather_kv0()

    def gather_kv_rest(after=()):
        for h in range(1, 4):
            if h == 1 and after:
                S.add("pool", lambda e: e.nop(), list(after), [])
            allgather2(kb[h], kg1[h], kg[h], f"K{h}")
            allgather2(vb[h], vg1[h], vg[h], f"V{h}")

    if "kT" in dbg:
        dump("kT", qkT, BF16, 4 * TPC)

    vfull.free()
    free_AD()
    stop_after = [x for x in dbg if x.startswith("stop")]
    stop_after = stop_after[0] if stop_after else None

    hoT = sbuf(BF16, 4 * TPC)
    if stop_after != "stopA":
        HT = 1024
        for b_ in pj[2:] + pT[1:]:
            b_.free()
        del pj[2:]
        del pT[1:]
        pA = Buf(PS, F32, 1024)
        pO = Buf(PS, F32, 1024)
        pU = bank()
        assert pA.off % 2048 == 0 and pO.off % 2048 == 0

        def two(bk, a, b):
            return bk.v(a, b)

        per_head = {}

        def hgrn_local(h):
            oloc = sbuf(F32, TPC)
            sg = sbuf(BF16, TPC)
            qS = sbuf(BF16, TPC)
            Sst = [sbuf(F32, 128) for _ in range(2)]
            Sbf = [sbuf(BF16, 128) for _ in range(2)]
            ctot = sbuf(F32, 16)
            S.add("dve", lambda e: e.memset(Sst[0].ap, 0.0), [], [Sst[0].v()])
            si = 0
            for u in range(2):
                c0 = u * HT
                W1, W2, W3, W4 = sbuf(F32, HT), sbuf(F32, HT), sbuf(F32, HT), sbuf(F32, HT)
                W5, W6 = sbuf(F32, HT), sbuf(F32, HT)
                qe, ke, kd, kdT, AmT = (sbuf(BF16, HT) for _ in range(5))
                bl = sbuf(F32, 16)
                cinc = sbuf(F32, 16)
                cexc = sbuf(F32, 16)
                for t2 in range(2):
                    tg = u * 2 + t2
                    ps = next_pj()
                    proj_fm(ps, wBv, (512 + h * 128, 512 + (h + 1) * 128), tg)
                    act(W1.v(t2 * 512, (t2 + 1) * 512), ps.v(0, 512), AF.Exp, scale=-1.0)
                act(W1.v(), W1.v(), AF.Ln, bias=1.0)
                act(W1.v(), W1.v(), AF.Exp, scale=-1.0)
                ts("dve", W1.v(), W1.v(), oml.v(h, h + 1), lb.v(h, h + 1), ALU.mult, ALU.add)
                act(W2.v(), W1.v(), AF.Ln)
                S.add("dve", lambda e, W3=W3, W2=W2: e.tensor_tensor_scan(W3.ap, rmask.ap, W2.ap, 0.0, ALU.mult, ALU.add),
                      [rmask.v(), W2.v()], [W3.v()])
                act(W4.v(), W1.v(), AF.Identity, scale=-1.0, bias=1.0)
                act(W5.v(), W3.v(), AF.Exp)
                act(W6.v(), W3.v(), AF.Exp, scale=-1.0)
                for t2 in range(2):
                    tg = u * 2 + t2
                    ps = next_pj()
                    proj_fm(ps, wBv, (h * 128, (h + 1) * 128), tg)
                    tt("dve", W2.v(t2 * 512, (t2 + 1) * 512), ps.v(0, 512), W5.v(t2 * 512, (t2 + 1) * 512), ALU.mult)
                Wg = sbuf(F32, HT)
                for t2 in range(2):
                    tg = u * 2 + t2
                    ps = next_pj()
                    proj_fm(ps, wBv, (1536 + h * 128, 1536 + (h + 1) * 128), tg)
                    act(Wg.v(t2 * 512, (t2 + 1) * 512), ps.v(0, 512), AF.Exp, scale=-1.0)
                    act(Wg.v(t2 * 512, (t2 + 1) * 512), Wg.v(t2 * 512, (t2 + 1) * 512), AF.Ln, bias=1.0)
                    act(Wg.v(t2 * 512, (t2 + 1) * 512), Wg.v(t2 * 512, (t2 + 1) * 512), AF.Exp, scale=-1.0)
                    tt("dve", sg.v(tg * 512, (tg + 1) * 512), ps.v(0, 512), Wg.v(t2 * 512, (t2 + 1) * 512), ALU.mult)
                Wg.free()
                cp("act", qe.v(), W2.v())
                tt("dve", W4.v(), W4.v(), W6.v(), ALU.mult)
                cp("act", ke.v(), W4.v())
                eb3 = W5.ap.rearrange("p (c t) -> p c t", t=64)
                tt("dve", kd.cv(kd.ap.rearrange("p (c t) -> p c t", t=64)),
                   W4.cv(W4.ap.rearrange("p (c t) -> p c t", t=64)),
                   W5.cv(eb3[:, :, 63:64].to_broadcast([128, 16, 64])), ALU.mult)
                b3 = W3.ap.rearrange("p (c t) -> p c t", t=64)
                cp("dve", bl.cv(bl.ap[:, 0:16].rearrange("p (c o) -> p c o", o=1)), W3.cv(b3[:, :, 63:64]))
                init = 0.0 if u == 0 else ctot.ap[:, 0:1]
                rd = [ones_f.v(0, 16), bl.v()] + ([] if u == 0 else [ctot.v(0, 1)])
                S.add("dve", lambda e, cinc=cinc, bl=bl, init=init: e.tensor_tensor_scan(
                    cinc.ap[:, 0:16], ones_f.ap[:, 0:16], bl.ap[:, 0:16], init, ALU.mult, ALU.add), rd, [cinc.v()])
                tt("dve", cexc.v(), cinc.v(), bl.v(), ALU.subtract)
                act(cexc.v(), cexc.v(), AF.Exp)
                cp("dve", ctot.v(0, 1), cinc.v(15, 16))
                tt("dve", qS.cv(qS.ap[:, c0:c0 + HT].rearrange("p (c t) -> p c t", t=64), [(c0, c0 + HT)]),
                   W2.cv(W2.ap.rearrange("p (c t) -> p c t", t=64)),
                   cexc.cv(cexc.ap[:, 0:16].rearrange("p (c o) -> p c o", o=1).to_broadcast([128, 16, 64])), ALU.mult)
                pt = pT[0]
                for n in range(8):
                    tr(pt.v(n * 128, (n + 1) * 128), kd.v(n * 128, (n + 1) * 128), ident.v())
                cp("act", kdT.v(), pt.v())
                for n in range(8):
                    mm(two(pA, n * 128, (n + 1) * 128), ke.v(n * 128, (n + 1) * 128), qe.v(n * 128, (n + 1) * 128))
                tt("dve", AmT.cv(AmT.ap.rearrange("p (n t) -> p n t", t=128)),
                   View(two(pA, 0, 1024).ap.rearrange("p (n t) -> p n t", t=128), "ps", two(pA, 0, 1024).regs),
                   hmask.cv(hmask.ap.rearrange("p (o t) -> p o t", o=1).to_broadcast([128, 8, 128])), ALU.mult)
                def hvv(m_, p0, p1):
                    gn_ = u * 8 + m_ // 2
                    return hv.v(gn_ * 512 + h * 128, gn_ * 512 + (h + 1) * 128, p0, p1)

                def uview(m_):
                    return (pU.v(0, 128), pA.v(0, 128), pA.v(512, 640))[m_ % 3]

                def emit_U(m_):
                    n_, r0 = m_ // 2, (m_ % 2) * 64
                    mm(uview(m_), kdT.v(n_ * 128, (n_ + 1) * 128, r0, r0 + 64), hvv(m_, r0, r0 + 64))

                emit_U(0)
                emit_U(1)
                for m in range(16):
                    n, cc = m // 2, m % 2
                    gm = u * 16 + m
                    col = n * 128 + cc * 64
                    if cc == 0:
                        mm(two(pO, n * 128, (n + 1) * 128), hvv(m, 0, 128), AmT.v(n * 128, (n + 1) * 128), start=True, stop=False)
                    if m + 2 < 16:
                        emit_U(m + 2)
                    if gm > 0:
                        mm(two(pO, col, col + 64), Sbf[si % 2].v(), qe.v(col, col + 64), start=False, stop=True)
                    so, sn = Sst[si % 2], Sst[(si + 1) % 2]
                    stt("dve", sn.v(), so.v(), W5.v(m * 64 + 63, m * 64 + 64), uview(m), ALU.mult, ALU.add)
                    si += 1
                    cp("act", Sbf[si % 2].v(), sn.v())
                cp("act", oloc.v(c0, c0 + HT), two(pO, 0, 1024))
                for b_ in (W1, W2, W3, W4, W5, W6, qe, ke, kd, kdT, AmT, bl, cinc, cexc):
                    b_.free()
            sfin = Sst[si % 2]
            aseg = sbuf(F32, 144)
            act(aseg.v(128, 129), ctot.v(0, 1), AF.Exp)
            cp("dve", aseg.v(0, 128), sfin.v())
            S.dma("pool", stb[h].v(), aseg.v(0, 129), f"stb{h}")
            allgather2(stb[h], stg1[h], stg[h], f"S{h}")
            for b_ in Sst + Sbf + [ctot, aseg]:
                b_.free()
            per_head[h] = (oloc, sg, qS)

        def hgrn_fix(h):
            oloc, sg, qS = per_head[h]
            run = sbuf(F32, 128)
            mine = sbuf(F32, 128)
            mine_bf = sbuf(BF16, 128)
            gl = [sbuf(F32, 132) for _ in range(2)]
            S.add("dve", lambda e: e.memset(run.ap, 0.0), [], [run.v()])
            S.add("dve", lambda e: e.memset(mine.ap, 0.0), [], [mine.v()])
            for r in range(NCORES - 1):
                g_ = gl[r % 2]
                S.dma("sp", g_.v(0, 129), stg[h].v(stg[h].ap[r * 128:(r + 1) * 128, :]), f"sgl{r % 2}")
                stt("dve", run.v(), run.v(), g_.v(128, 129), g_.v(0, 128), ALU.mult, ALU.add)
                stt("dve", mine.v(), run.v(), sel.v(r + 1, r + 2), mine.v(), ALU.mult, ALU.add)
            cp("act", mine_bf.v(), mine.v())
            for tg in range(4):
                a, b = tg * 512, (tg + 1) * 512
                ps = next_pj()
                mm(ps.v(0, 512), mine_bf.v(), qS.v(a, b))
                tt("dve", oloc.v(a, b), oloc.v(a, b), ps.v(0, 512), ALU.add)
                sq = tmp1[tg % 2]
                act(sq.v(), oloc.v(a, b), AF.Square)
                ps2 = next_pj()
                mm(ps2.v(0, 512), ones_f.v(), sq.v())
                t2_ = tmp2[tg % 2]
                act(t2_.v(), ps2.v(0, 512), AF.Ln, scale=1.0 / 128, bias=1e-6)
                act(t2_.v(), t2_.v(), AF.Exp, scale=-0.5)
                tt("dve", t2_.v(), oloc.v(a, b), t2_.v(), ALU.mult)
                stt("dve", hoT.v(h * TPC + a, h * TPC + b), t2_.v(), hnw.v(0, 1), sg.v(a, b), ALU.mult, ALU.mult)
            S.dma("pool", hob[h].v(), hoT.v(h * TPC, (h + 1) * TPC), "hob")
            for b_ in (oloc, sg, qS, run, mine, mine_bf) + tuple(gl):
                b_.free()

        hgrn_local(0)
        hgrn_local(1)
        hgrn_fix(0)
        hgrn_local(2)
        hgrn_fix(1)
        hgrn_local(3)
        for b_ in [pA, pO, pU]:
            b_.free()
        pj.extend([bank(), bank()])
        pT.append(bank(BF16))
        hv.free()
        wB.free()

        if "hoT" in dbg:
            dump("hoT", hoT, BF16, 4 * TPC)

    if stop_after not in ("stopA", "stopC"):
        alloc_AD()
        qkT = PH["qkT"]
        for kc in range(8):
            st = load_w_cast(None, 0, [w_in[kc * 128:(kc + 1) * 128, 2048:2560],
                                       w_in[kc * 128:(kc + 1) * 128, 3584:4096]], 1024, kc)
            cast_group_qk(PH['wA'], wAv, kc, st)
        S.dma("sp", PH["rope"].v(), ropeq_d, "rope")
        cnt = norm_transpose_all(xi, cnt)
        rope_proj(wAv, qkT, ri)
        if stop_after != "stopA":
            hgrn_fix(2)
            hgrn_fix(3)
        sgd = sbuf(BF16, 4 * TPC)
        for h in range(4):
            for tg in range(4):
                ps = next_pj()
                proj_fm(ps, wAv, (1024 + h * 128, 1024 + (h + 1) * 128), tg)
                t1 = tmp1[tg % 2]
                act(t1.v(), ps.v(0, 512), AF.Exp, scale=-1.0)
                act(t1.v(), t1.v(), AF.Ln, bias=1.0)
                act(t1.v(), t1.v(), AF.Exp, scale=-1.0)
                tt("dve", sgd.v(h * TPC + tg * 512, h * TPC + (tg + 1) * 512), ps.v(0, 512), t1.v(), ALU.mult)
        if "qT" in dbg:
            dump("qT", qkT, BF16, 4 * TPC)
        gather_kv_rest(after=[b_.v() for b_ in PH["xs"]] + [PH["wA"].v()])
        for h in range(4):
            allgather2(hob[h], hog1[h], hog[h], f"H{h}")
        xs_keep = PH["xs"][0]
        for b_ in PH["xs"][1:]:
            b_.free()
        free_AD(keep=("xs", "qkT"))
        for b_ in [hT] + pT + pj + wst[1:]:
            b_.free()
        wst_keep = wst[0]

    if stop_after not in ("stopA", "stopC", "stopD"):
        for b_ in tmp1 + tmp2 + [rmask]:
            b_.free()
        od = sbuf(F32, 4 * TPC)
        mixd = sbuf(BF16, 4 * TPC)
        RK = 4
        kring = [[sbuf(BF16, 1024) for _ in range(RK)] for _ in range(2)]
        for c_ in range(2):
            for b_ in kring[c_]:
                S.add("dve", lambda e, b_=b_: e.memset(b_.ap, 0.0), [], [b_.v()])
        vring = [sbuf(BF16, 1024) for _ in range(RK)]
        RE = 7
        ering = [sbuf(BF16, 1024) for _ in range(RE)]
        pS2 = [Buf(PS, F32, 1024), Buf(PS, F32, 1024)]
        for pr in pS2:
            assert pr.off % 2048 == 0
        pacc = [bank(), bank()]
        pden = [bank(), bank()]
        ft = [sbuf(F32, 512) for _ in range(4)]
        esum = [sbuf(F32, 1024) for _ in range(2)]
        etmp = [sbuf(BF16, 1024) for _ in range(2)]

        sbs = []
        for h in range(4):
            for g in range(4):
                for sb_ in range(4 * g + 4):
                    sbs.append((h, g, sb_))
        tiles = []
        for i, (h, g, sb_) in enumerate(sbs):
            for kt in range(8):
                tiles.append((i, kt))

        def load_sb(i):
            h, g, sb_ = sbs[i]
            rank, sl = sb_ // 2, sb_ % 2
            row = rank * 128
            for c_ in range(2):
                S.dma("sp", kring[c_][i % RK].v(0, 1024, c_ * 64, (c_ + 1) * 64),
                      kg[h].v(kg[h].ap[row + c_ * 64:row + (c_ + 1) * 64, sl * 1024:(sl + 1) * 1024]), f"kr{c_}{i % RK}")
            vrow = (rank * 2 + sl) * 128
            S.dma("sp", vring[i % RK].v(), vg[h].v(vg[h].ap[vrow:vrow + 128, :]), f"vr{i % RK}")

        def tile_info(ti):
            i, kt = tiles[ti]
            h, g, sb_ = sbs[i]
            r = sb_ - 4 * g
            c0 = 128 * r if r > 0 else 0
            return i, kt, h, g, sb_, r, c0

        def qk(ti):
            i, kt, h, g, sb_, r, c0 = tile_info(ti)
            pr2 = pS2[ti % 2]
            q0 = h * TPC + g * 512
            for c in range(2):
                ks = kring[c][i % RK]
                first = True
                if r >= 0:
                    mm(pr2.v(c * 512 + 128 * r, c * 512 + 128 * r + 128), ident.v(), amask.v(kt * 128, (kt + 1) * 128), start=True, stop=False)
                    first = False
                mm(pr2.v(c * 512 + c0, c * 512 + 512), ks.v(kt * 128, (kt + 1) * 128), qkT.v(q0 + c0, q0 + 512),
                   start=first, stop=True)

        def ex(ti):
            i, kt, h, g, sb_, r, c0 = tile_info(ti)
            pr2 = pS2[ti % 2]
            es = ering[ti % RE]
            src = pr2.cv(pr2.ap.rearrange("p (c q) -> p c q", c=2)[:, :, c0:512], [(c0, 512), (512 + c0, 1024)])
            dst = es.cv(es.ap.rearrange("p (c q) -> p c q", c=2)[:, :, c0:512], [(c0, 512), (512 + c0, 1024)])
            act(dst, src, AF.Exp, scale=0.125)

        def pv(ti):
            i, kt, h, g, sb_, r, c0 = tile_info(ti)
            es = ering[ti % RE]
            vs_ = vring[i % RK]
            first = (sb_ == 0 and kt == 0)
            last = (sb_ == 4 * g + 3 and kt == 7)
            gi = h * 4 + g
            eb_ = esum[gi % 2]
            for c in range(2):
                mm(pacc[c].v(c0, 512), vs_.v(kt * 128, (kt + 1) * 128), es.v(c * 512 + c0, c * 512 + 512), start=first, stop=last)
            rg_ = [(c0, 512), (512 + c0, 1024)]
            e3 = es.cv(es.ap.rearrange("p (c q) -> p c q", c=2)[:, :, c0:512], rg_)
            tb_ = etmp[i % 2]
            t3 = tb_.cv(tb_.ap.rearrange("p (c q) -> p c q", c=2)[:, :, c0:512], rg_)
            s3 = eb_.cv(eb_.ap.rearrange("p (c q) -> p c q", c=2)[:, :, c0:512], rg_)
            if kt == 0:
                cp("dve", t3, e3)
            else:
                tt("dve", t3, e3, t3, ALU.add)
            if kt == 7:
                if sb_ == 0:
                    cp("dve", s3, t3)
                else:
                    tt("dve", s3, t3, s3, ALU.add)
            if last:
                for c in range(2):
                    mm(pden[c].v(0, 512), ones_f.v(), eb_.v(c * 512, (c + 1) * 512))
                    act(ft[c].v(), pden[c].v(0, 512), AF.Ln)
                    act(ft[c].v(), ft[c].v(), AF.Exp, scale=-1.0)
                tt("dve", ft[2].v(), pacc[0].v(0, 512), ft[0].v(), ALU.mult)
                tt("dve", ft[3].v(), pacc[1].v(0, 512), ft[1].v(), ALU.mult)
                o0 = h * TPC + g * 512
                stt("dve", od.v(o0, o0 + 512), ft[3].v(), neglam.v(0, 1), ft[2].v(), ALU.mult, ALU.add)

        FP = {}

        def phaseF_prep():
            woutbf = sbuf(BF16, 8 * D)
            for kc in range(8):
                S.dma("sp", wst_keep.v(), w_out[kc * 128:(kc + 1) * 128, :], "wst0")
                cp("act" if kc % 2 else "dve", woutbf.v(kc * D, (kc + 1) * D), wst_keep.v())
                yield
            hoR = [sbuf(BF16, NT * 128) for _ in range(4)]
            for h in range(4):
                hoR4 = hoR[h].ap.rearrange("p (a s t) -> p a s t", a=NCORES, s=2)
                for s_ in range(2):
                    def _ld(e, h=h, s_=s_, hoR4=hoR4):
                        pid = 0 if "nodyn" in dbg else PID["sp"]
                        src = hog[h].ap[:, bass.ds(pid * 128 + s_ * 1024, 128)]
                        return e.dma_start(out=hoR4[:, :, s_, :], in_=src.rearrange("(a e) t -> e a t", a=NCORES))
                    S.add("sp", _ld, [hog[h].v()], [hoR[h].v()], kind="d", group="hoR")
                yield
            FP.update(woutbf=woutbf, hoR=hoR)

        FPG = [phaseF_prep()]
        LA = 2
        for i in range(min(LA, len(sbs))):
            load_sb(i)
        qk(0)
        for ti in range(len(tiles)):
            i, kt = tiles[ti]
            if kt == 0 and i + LA < len(sbs):
                load_sb(i + LA)
            if kt == 0 and sbs[i][0] >= 2 and FPG:
                try:
                    next(FPG[0])
                except StopIteration:
                    FPG.clear()
            if ti + 1 < len(tiles):
                qk(ti + 1)
            ex(ti)
            pv(ti)
        for b_ in kring[0] + kring[1] + vring + ering + pS2 + pacc + pden + ft + esum + etmp:
            b_.free()
        if "od" in dbg:
            dump("od", od, F32, 4 * TPC)

        pn = [bank() for _ in range(3)]
        nt_ = [sbuf(F32, 512) for _ in range(4)]
        grp = [(h, g) for h in range(4) for g in range(4)]

        def pp_a(k_):
            h, g = grp[k_]
            a, b = h * TPC + g * 512, h * TPC + (g + 1) * 512
            sq, ps = nt_[(k_ % 2) * 2], pn[k_ % 3]
            act(sq.v(), od.v(a, b), AF.Square)
            mm(ps.v(0, 512), ones_f.v(), sq.v())

        def pp_b(k_):
            h, g = grp[k_]
            a, b = h * TPC + g * 512, h * TPC + (g + 1) * 512
            t2_, ps = nt_[(k_ % 2) * 2 + 1], pn[k_ % 3]
            act(t2_.v(), ps.v(0, 512), AF.Ln, scale=1.0 / 128, bias=1e-5)
            act(t2_.v(), t2_.v(), AF.Exp, scale=-0.5)
            tt("dve", t2_.v(), od.v(a, b), t2_.v(), ALU.mult)
            stt("dve", mixd.v(a, b), t2_.v(), dnw.v(0, 1), sgd.v(a, b), ALU.mult, ALU.mult)

        pp_a(0)
        for k_ in range(16):
            if k_ + 1 < 16:
                pp_a(k_ + 1)
            pp_b(k_)
        for b_ in pn + nt_:
            b_.free()
        od.free()
        sgd.free()
        qkT.free()
        if "mixd" in dbg:
            dump("mixd", mixd, BF16, 4 * TPC)

        woutbf, hoR = FP["woutbf"], FP["hoR"]
        po = [bank() for _ in range(4)]
        yb = [sbuf(F32, D) for _ in range(2)]
        ob = [sbuf(F32, D) for _ in range(2)]
        xr = [xs_keep, sbuf(F32, D)]
        fst = [sbuf(F32, 16) for _ in range(2)]
        junk2 = sbuf(BF16, D)
        def pf_a(jj):
            xb = xr[jj % 2]
            S.dma("sp", xb.v(), xi[jj * 128:(jj + 1) * 128, :], f"xr{jj % 2}")
            for half in range(2):
                ps = po[(jj % 2) * 2 + half]
                for kc in range(8):
                    if kc < 4:
                        l = hoR[kc].v(jj * 128, (jj + 1) * 128)
                    else:
                        l = mixd.v((kc - 4) * TPC + jj * 128, (kc - 4) * TPC + (jj + 1) * 128)
                    mm(ps.v(0, 512), l, woutbf.v(kc * D + half * 512, kc * D + (half + 1) * 512),
                       start=(kc == 0), stop=(kc == 7))
                tt("dve", yb[jj % 2].v(half * 512, (half + 1) * 512), xb.v(half * 512, (half + 1) * 512), ps.v(0, 512), ALU.add)
            act(junk2.v(), yb[jj % 2].v(), AF.Square, accum=fst[jj % 2].v(0, 1))

        def pf_b(jj):
            st = fst[jj % 2]
            act(st.v(2, 3), st.v(0, 1), AF.Ln, scale=1.0 / D, bias=1e-6)
            act(st.v(3, 4), st.v(2, 3), AF.Exp, scale=-0.5)
            stt("dve", ob[jj % 2].v(), yb[jj % 2].v(), st.v(3, 4), fnw.v(), ALU.mult, ALU.mult)
            S.dma("sp", y_d[jj * 128:(jj + 1) * 128, :], ob[jj % 2].v(), "out")

        pf_a(0)
        for jj in range(NT):
            if jj + 1 < NT:
                pf_a(jj + 1)
            pf_b(jj)

    S.analyze()
    esem = {e: nc.alloc_semaphore(f"s_{e}") for e in S.ENGS}
    gsem = {g: nc.alloc_semaphore(f"g_{i}") for i, g in enumerate(S.groups)}
    with nc.Block() as block:
        @block.tensor
        def _(h):
            S.emit_engine("pe", h, esem, gsem)

        @block.scalar
        def _(h):
            S.emit_engine("act", h, esem, gsem)

        @block.vector
        def _(h):
            S.emit_engine("dve", h, esem, gsem)

        @block.gpsimd
        def _(h):
            S.emit_engine("pool", h, esem, gsem)

        @block.sync
        def _(h):
            PID["sp"] = h.partition_id()
            S.emit_engine("sp", h, esem, gsem)
            for g, arr in S.gpos.items():
                h.wait_ge(gsem[g], arr[-1][1])
    return nc, S, (SB.peak, PS.peak)


def _rope_table(pos):
    half = 32
    inv_freq = (1.0 / (np.float32(10000.0) ** (np.arange(half, dtype=np.float32) / np.float32(half)))).astype(np.float32)
    ang = pos.astype(np.float32)[None, :] * inv_freq[:, None]
    c, s = np.cos(ang).astype(np.float32), np.sin(ang).astype(np.float32)
    ct = np.concatenate([c, c, c, c], 0)
    stb = np.concatenate([-s, s, -s, s], 0)
    return np.ascontiguousarray(np.concatenate([ct, stb], 1))


def make_in_maps(x, norm_w, w_in, hgrn_lb_logits, hgrn_norm_w, diff_lambda_q1, diff_lambda_k1,
                 diff_lambda_q2, diff_lambda_k2, diff_norm_w, w_out, final_norm_w):
    f32 = np.float32
    x2 = np.asarray(x, f32).reshape(T, D)
    w_in2 = np.ascontiguousarray(np.asarray(w_in, f32).reshape(D, 4096))
    w_out2 = np.ascontiguousarray(np.asarray(w_out, f32).reshape(D, D))
    nw = np.ascontiguousarray(np.asarray(norm_w, f32).reshape(8, 128).T)
    lbl = np.ascontiguousarray(np.asarray(hgrn_lb_logits, f32).reshape(2, 4, 128).transpose(2, 0, 1).reshape(128, 8))
    hnw = np.ascontiguousarray(np.asarray(hgrn_norm_w, f32).reshape(128, 1))
    dnw = np.ascontiguousarray(np.asarray(diff_norm_w, f32).reshape(128, 1))
    lamv = np.concatenate([np.asarray(a, f32).reshape(64) for a in
                           (diff_lambda_q1, diff_lambda_k1, diff_lambda_q2, diff_lambda_k2)])
    lamv = np.ascontiguousarray(np.broadcast_to(lamv[None, :], (128, 256)))
    fnw = np.ascontiguousarray(np.broadcast_to(np.asarray(final_norm_w, f32).reshape(1, D), (128, D)))
    ident = np.eye(128, dtype=f32).astype(ml_dtypes.bfloat16)
    p = np.arange(128)
    hmask = ((p[:, None] // 64 == p[None, :] // 64) & (p[:, None] <= p[None, :])).astype(f32)
    xt = x2.reshape(128, 128, D)
    maps = []
    for c in range(NCORES):
        xc = np.ascontiguousarray(x2[c * TPC:(c + 1) * TPC])
        tiles = c + 8 * np.arange(NT)
        xi = np.ascontiguousarray(xt[tiles].reshape(TPC, D))
        posk = np.arange(c * TPC, (c + 1) * TPC)
        posq = (tiles[:, None] * 128 + np.arange(128)[None, :]).reshape(-1)
        am = np.zeros((128, 8, 128), f32)
        for m in range(8):
            if m > c:
                am[:, m, :] = NEG
            elif m == c:
                am[:, m, :] = np.where(p[:, None] <= p[None, :], 0.0, NEG)
        sel = np.zeros((128, 8), f32)
        sel[:, c] = 1.0
        maps.append({
            "xc": xc, "xi": xi, "w_in": w_in2, "w_out": w_out2, "nw": nw, "lbl": lbl, "hnw": hnw,
            "dnw": dnw, "lamv": lamv, "fnw": fnw, "ropek": _rope_table(posk), "ropeq": _rope_table(posq),
            "amask": np.ascontiguousarray(am.reshape(128, 1024)).astype(ml_dtypes.bfloat16),
            "hmask": hmask, "ident": ident, "sel": sel,
        })
    return maps


_PROG = {}


def kernel(**inputs):
    if "p" not in _PROG:
        _PROG["p"] = build_program()
    nc = _PROG["p"][0]
    maps = make_in_maps(**inputs)
    res = run_bass_kernel_spmd(nc, maps, core_ids=list(range(NCORES)))
    out = np.empty((128, 128, D), np.float32)
    for c in range(NCORES):
        yc = np.asarray(res.results[c]["y"], np.float32).reshape(NT, 128, D)
        out[c + 8 * np.arange(NT)] = yc
    return out.reshape(1, T, D)
```

```python
import numpy as np
import ml_dtypes
import concourse.bass as bass
import concourse.mybir as mybir
from concourse.bass_utils import run_bass_kernel_spmd

F32 = mybir.dt.float32
BF16 = mybir.dt.bfloat16
AF = mybir.ActivationFunctionType
ALU = mybir.AluOpType
AX = mybir.AxisListType
ESZ = {F32: 4, BF16: 2}
GRAN = 64

NCORES = 8
T = 16384
D = 1024
TPC = 2048
NT = 16
LAMBDA_INIT = 0.8 - 0.6 * 1.0
NEG = -30000.0


class View:
    __slots__ = ("ap", "space", "regs")

    def __init__(self, ap, space, regs):
        self.ap, self.space, self.regs = ap, space, regs


class Arena:
    def __init__(self, base_ap_f32, space, cap):
        self.base, self.space, self.cap = base_ap_f32, space, cap
        self.free = [(0, cap)]
        self.peak = 0

    def alloc(self, nbytes):
        nbytes = (nbytes + GRAN - 1) // GRAN * GRAN
        for i, (o, n) in enumerate(self.free):
            if n >= nbytes:
                if n == nbytes:
                    self.free.pop(i)
                else:
                    self.free[i] = (o + nbytes, n - nbytes)
                self.peak = max(self.peak, o + nbytes)
                return o, nbytes
        raise MemoryError(f"{self.space} arena full: want {nbytes}, free={self.free}")

    def release(self, off, nbytes):
        self.free.append((off, nbytes))
        self.free.sort()
        m = []
        for o, n in self.free:
            if m and m[-1][0] + m[-1][1] == o:
                m[-1] = (m[-1][0], m[-1][1] + n)
            else:
                m.append((o, n))
        self.free = m


class Buf:
    def __init__(self, arena, dtype, n):
        self.arena, self.dtype, self.n = arena, dtype, n
        self.esz = ESZ[dtype]
        self.off, self.nbytes = arena.alloc(n * self.esz)
        a = arena.base[:, self.off // 4:(self.off + self.nbytes) // 4]
        if dtype != F32:
            a = a.bitcast(dtype)
        self.ap = a[:, 0:n]
        self.space = arena.space

    def free(self):
        self.arena.release(self.off, self.nbytes)

    def v(self, a=0, b=None, p0=0, p1=128):
        if b is None:
            b = self.n
        return View(self.ap[p0:p1, a:b], self.space,
                    [(self.off + a * self.esz, self.off + b * self.esz)])

    def cv(self, ap, ranges=None):
        if ranges is None:
            ranges = [(0, self.n)]
        return View(ap, self.space,
                    [(self.off + a * self.esz, self.off + b * self.esz) for a, b in ranges])


class DT:
    def __init__(self, nc, name, shape, dtype, kind=None):
        if kind is None:
            self.t = nc.dram_tensor(name, shape, dtype)
        else:
            self.t = nc.dram_tensor(name, shape, dtype, kind=kind)
        self.ap = self.t.ap()
        self.space = "dram_" + name

    def v(self, ap=None):
        return View(self.ap if ap is None else ap, self.space, [(0, GRAN)])


class Op:
    __slots__ = ("eng", "fn", "reads", "writes", "kind", "group", "idx", "deps", "signal", "inc")


class Sched:
    ENGS = ("pe", "act", "dve", "pool", "sp")

    def __init__(self):
        self.ops = []
        self.groups = {}

    def add(self, eng, fn, reads=(), writes=(), kind="c", group=None, inc=16):
        op = Op()
        op.eng, op.fn, op.reads, op.writes = eng, fn, list(reads), list(writes)
        op.kind, op.group, op.idx = kind, group, len(self.ops)
        op.deps, op.signal, op.inc = {}, False, inc
        if kind != "c":
            self.groups.setdefault(group, []).append(op.idx)
        self.ops.append(op)
        return op

    def dma(self, eng, out, in_, group, **kw):
        r = [in_] if isinstance(in_, View) else []
        w = [out] if isinstance(out, View) else []
        oa = out.ap if isinstance(out, View) else out
        ia = in_.ap if isinstance(in_, View) else in_
        return self.add(eng, lambda e: e.dma_start(out=oa, in_=ia, **kw), r, w, kind="d", group=group)

    def analyze(self):
        lastw, readers = {}, {}
        ops = self.ops
        for op in ops:
            deps = {}
            rblocks, wblocks = [], []
            for v in op.reads:
                g_ = 2048 if v.space == "ps" else GRAN
                for lo, hi in v.regs:
                    for blk in range(lo // g_, (hi - 1) // g_ + 1):
                        rblocks.append((v.space, blk))
                        if v.space == "ps":
                            wblocks.append((v.space, blk))
            for v in op.writes:
                g_ = 2048 if v.space == "ps" else GRAN
                for lo, hi in v.regs:
                    for blk in range(lo // g_, (hi - 1) // g_ + 1):
                        wblocks.append((v.space, blk))
            for key in rblocks:
                w = lastw.get(key)
                if w is not None:
                    deps[w] = True
            for key in wblocks:
                w = lastw.get(key)
                if w is not None and w not in deps:
                    deps[w] = False
                rl = readers.get(key)
                if rl:
                    for r in rl:
                        if r not in deps:
                            deps[r] = False
            for key in rblocks:
                rl = readers.get(key)
                if not rl:
                    readers[key] = [op.idx]
                elif rl[-1] != op.idx:
                    rl.append(op.idx)
            for key in wblocks:
                lastw[key] = op.idx
                readers[key] = []
            deps.pop(op.idx, None)
            chan = {}
            for p, raw in deps.items():
                po = ops[p]
                if po.kind == "c":
                    if po.eng == op.eng and op.kind == "c":
                        if po.eng == "pe" or not raw:
                            continue
                    key = ("e", po.eng)
                else:
                    key = ("g", po.group)
                if key not in chan or chan[key] < p:
                    chan[key] = p
            op.deps = chan
            for p in chan.values():
                ops[p].signal = True
        tickets = {}
        cnt = {e: 0 for e in self.ENGS}
        for op in ops:
            if op.kind == "c" and op.signal:
                cnt[op.eng] += 1
                tickets[op.idx] = cnt[op.eng]
        self.tickets, self.counts = tickets, cnt
        self.gpos = {}
        for g, lst in self.groups.items():
            acc, arr = 0, []
            for i in lst:
                acc += ops[i].inc
                arr.append((i, acc))
            self.gpos[g] = arr

    def gcount_before(self, g, idx):
        v = 0
        for i, acc in self.gpos[g]:
            if i < idx:
                v = acc
            else:
                break
        return v

    def emit_engine(self, engname, h, esem, gsem):
        waited = {}
        ops = self.ops
        for op in ops:
            if op.eng != engname:
                continue
            for key, p in op.deps.items():
                if key[0] == "e":
                    sem, val = esem[key[1]], self.tickets[p]
                else:
                    sem, val = gsem[key[1]], self.gcount_before(key[1], op.idx)
                if waited.get(key, 0) >= val:
                    continue
                waited[key] = val
                h.wait_ge(sem, val)
            ins = op.fn(h)
            if op.kind == "c":
                if op.signal:
                    ins.then_inc(esem[engname], 1)
            else:
                ins.then_inc(gsem[op.group], op.inc)


def build_program(dbg=()):
    nc = bass.Bass("TRN2", target_bir_lowering=False)
    S = Sched()
    PID = {}

    def din(name, shape, dt):
        return nc.dram_tensor(name, shape, dt, kind="ExternalInput").ap()

    xc = din("xc", [TPC, D], F32)
    xi = din("xi", [TPC, D], F32)
    w_in = din("w_in", [D, 4096], F32)
    w_out = din("w_out", [D, D], F32)
    nw_d = din("nw", [128, 8], F32)
    lbl_d = din("lbl", [128, 8], F32)
    hnw_d = din("hnw", [128, 1], F32)
    dnw_d = din("dnw", [128, 1], F32)
    lamv_d = din("lamv", [128, 256], F32)
    fnw_d = din("fnw", [128, D], F32)
    ropek_d = din("ropek", [128, 2 * TPC], F32)
    ropeq_d = din("ropeq", [128, 2 * TPC], F32)
    amask_d = din("amask", [128, 8 * 128], BF16)
    hmask_d = din("hmask", [128, 128], F32)
    ident_d = din("ident", [128, 128], BF16)
    sel_d = din("sel", [128, 8], F32)
    y_d = nc.dram_tensor("y", [TPC, D], F32, kind="ExternalOutput").ap()
    dbg_out = {}

    kb = [DT(nc, f"kb{h}", [128, TPC], BF16) for h in range(4)]
    kg1 = [DT(nc, f"kg1{h}", [4 * 128, TPC], BF16) for h in range(4)]
    kg = [DT(nc, f"kg{h}", [NCORES * 128, TPC], BF16) for h in range(4)]
    vb = [DT(nc, f"vb{h}", [256, 1024], BF16) for h in range(4)]
    vg1 = [DT(nc, f"vg1{h}", [4 * 256, 1024], BF16) for h in range(4)]
    vg = [DT(nc, f"vg{h}", [NCORES * 256, 1024], BF16) for h in range(4)]
    stb = [DT(nc, f"stb{h}", [128, 129], F32) for h in range(4)]
    stg1 = [DT(nc, f"stg1{h}", [4 * 128, 129], F32) for h in range(4)]
    stg = [DT(nc, f"stg{h}", [NCORES * 128, 129], F32) for h in range(4)]
    hob = [DT(nc, f"hob{h}", [128, TPC], BF16) for h in range(4)]
    hog1 = [DT(nc, f"hog1{h}", [4 * 128, TPC], BF16) for h in range(4)]
    hog = [DT(nc, f"hog{h}", [NCORES * 128, TPC], BF16) for h in range(4)]
    RG_DIE = [[0, 1, 2, 3], [4, 5, 6, 7]]
    RG_PAIR = [[0, 4], [1, 5], [2, 6], [3, 7]]

    def allgather2(src, mid, dst, name):
        S.add("pool", lambda e: e.collective_compute("AllGather", ALU.bypass, replica_groups=RG_DIE,
                                                     ins=[src.ap.opt()], outs=[mid.ap.opt()]),
              [src.v()], [mid.v()], kind="cc", group="cc1" + name, inc=1)
        S.add("pool", lambda e: e.collective_compute("AllGather", ALU.bypass, replica_groups=RG_PAIR,
                                                     ins=[mid.ap.opt()], outs=[dst.ap.opt()]),
              [mid.v()], [dst.v()], kind="cc", group="cc2" + name, inc=1)

    SB_CAP = 206 * 1024
    sb_t = nc.alloc_sbuf_tensor("sbuf_all", [128, SB_CAP // 4], F32)
    ps_t = nc.alloc_psum_tensor("psum_all", [128, 4096], F32)
    SB = Arena(sb_t.ap(), "sb", SB_CAP)
    PS = Arena(ps_t.ap(), "ps", 16384)

    def sbuf(dt, n):
        return Buf(SB, dt, n)

    def bank(dt=F32):
        return Buf(PS, dt, 2048 // ESZ[dt])

    def act(o, i, func, scale=1.0, bias=None, accum=None, eng="act"):
        kw = {}
        if bias is not None:
            kw["bias"] = bias.ap if isinstance(bias, View) else bias
        if accum is not None:
            kw["accum_out"] = accum.ap
        sc = scale.ap if isinstance(scale, View) else scale
        r = [i] + [x for x in (bias, scale) if isinstance(x, View)]
        w = [o] + ([accum] if accum is not None else [])
        S.add("act", lambda e: e.activation(o.ap, i.ap, func, scale=sc, **kw), r, w)

    def ts(eng, o, i, s1, s2, op0, op1=None):
        a1 = s1.ap if isinstance(s1, View) else s1
        a2 = s2.ap if isinstance(s2, View) else s2
        r = [i] + [x for x in (s1, s2) if isinstance(x, View)]
        if op1 is None:
            S.add(eng, lambda e: e.tensor_scalar(o.ap, i.ap, a1, None, op0), r, [o])
        else:
            S.add(eng, lambda e: e.tensor_scalar(o.ap, i.ap, a1, a2, op0, op1), r, [o])

    def tt(eng, o, a, b, op):
        S.add(eng, lambda e: e.tensor_tensor(o.ap, a.ap, b.ap, op), [a, b], [o])

    def stt(eng, o, a, s, b, op0, op1):
        sa = s.ap if isinstance(s, View) else s
        r = [a, b] + ([s] if isinstance(s, View) else [])
        S.add(eng, lambda e: e.scalar_tensor_tensor(o.ap, a.ap, sa, b.ap, op0, op1), r, [o])

    def cp(eng, o, i):
        if eng == "act":
            S.add("act", lambda e: e.activation(o.ap, i.ap, AF.Copy), [i], [o])
        else:
            S.add(eng, lambda e: e.tensor_copy(o.ap, i.ap), [i], [o])

    def mm(o, l, r, start=True, stop=True):
        S.add("pe", lambda e: e.matmul(o.ap, l.ap, r.ap, start=start, stop=stop), [l, r], [o])

    def tr(o, i, idv):
        S.add("pe", lambda e: e.transpose(o.ap, i.ap, idv.ap), [i, idv], [o])

    def dump(name, buf, dt, n):
        d = nc.dram_tensor("dbg_" + name, [128, n], dt, kind="ExternalOutput").ap()
        dbg_out[name] = True
        S.dma("sp", d, buf.v(0, n), "dbg")

    ident = sbuf(BF16, 128)
    ones_bf = sbuf(BF16, 128)
    ones_f = sbuf(F32, 128)
    hmask = sbuf(F32, 128)
    amask = sbuf(BF16, 1024)
    rmask = sbuf(F32, 1024)
    nwt = sbuf(F32, 16)
    lbl = sbuf(F32, 16)
    lb = sbuf(F32, 16)
    oml = sbuf(F32, 16)
    hnw = sbuf(F32, 16)
    dnw = sbuf(F32, 16)
    sel = sbuf(F32, 16)
    lamv = sbuf(F32, 256)
    lamt = sbuf(F32, 128)
    ls = sbuf(F32, 16)
    neglam = sbuf(F32, 16)
    fnw = sbuf(F32, D)
    for dst, src, n in ((ident, ident_d, 128), (hmask, hmask_d, 128), (amask, amask_d, 1024),
                        (nwt, nw_d, 8), (lbl, lbl_d, 8), (hnw, hnw_d, 1), (dnw, dnw_d, 1),
                        (sel, sel_d, 8), (lamv, lamv_d, 256), (fnw, fnw_d, D)):
        S.dma("sp", dst.v(0, n), src, "const")
    S.add("pool", lambda e: e.memset(ones_bf.ap, 1.0), [], [ones_bf.v()])
    S.add("pool", lambda e: e.memset(ones_f.ap, 1.0), [], [ones_f.v()])
    S.add("pool", lambda e: e.memset(rmask.ap, 1.0), [], [rmask.v()])
    S.add("pool", lambda e: e.memset(rmask.ap.rearrange("p (c t) -> p c t", t=64)[:, :, 0:1], 0.0),
          [], [rmask.v()])
    tt("dve", lb.v(0, 4), lbl.v(4, 8), lbl.v(0, 4), ALU.subtract)
    act(lb.v(0, 4), lb.v(0, 4), AF.Exp)
    ts("dve", lb.v(0, 4), lb.v(0, 4), 1.0, None, ALU.add)
    S.add("dve", lambda e: e.reciprocal(lb.ap[:, 0:4], lb.ap[:, 0:4]), [lb.v(0, 4)], [lb.v(0, 4)])
    ts("dve", oml.v(0, 4), lb.v(0, 4), -1.0, 1.0, ALU.mult, ALU.add)
    ts("dve", dnw.v(0, 1), dnw.v(0, 1), 1.0 - LAMBDA_INIT, None, ALU.mult)
    tt("dve", lamt.v(0, 64), lamv.v(0, 64), lamv.v(64, 128), ALU.mult)
    tt("dve", lamt.v(64, 128), lamv.v(128, 192), lamv.v(192, 256), ALU.mult)
    S.add("dve", lambda e: e.reduce_sum(ls.ap[:, 0:1], lamt.ap[:, 0:64], AX.X), [lamt.v(0, 64)], [ls.v(0, 1)])
    S.add("dve", lambda e: e.reduce_sum(ls.ap[:, 1:2], lamt.ap[:, 64:128], AX.X), [lamt.v(64, 128)], [ls.v(1, 2)])
    act(ls.v(0, 2), ls.v(0, 2), AF.Exp)
    tt("dve", neglam.v(0, 1), ls.v(1, 2), ls.v(0, 1), ALU.subtract)
    ts("dve", neglam.v(0, 1), neglam.v(0, 1), -LAMBDA_INIT, None, ALU.add)

    wst = [sbuf(F32, 1024) for _ in range(2)]
    wst_i = [0]

    def load_w_cast(dst, dcol, src_ap, n, kc, swap_to=None):
        i = wst_i[0] % 2
        wst_i[0] += 1
        st = wst[i]
        if isinstance(src_ap, list):
            for j, sa in enumerate(src_ap):
                S.dma("sp", st.v(j * 512, (j + 1) * 512), sa, f"wst{i}")
        else:
            S.dma("sp", st.v(0, n), src_ap, f"wst{i}")
        return st

    wsc_i = [0]

    def wscale(dst_view, src_view, kc):
        wsc_i[0] += 1
        if wsc_i[0] % 2:
            act(dst_view, src_view, AF.Copy, scale=nwt.v(kc, kc + 1))
        else:
            ts("dve", dst_view, src_view, nwt.v(kc, kc + 1), None, ALU.mult)

    hT = sbuf(BF16, 8 * TPC)
    hT3 = hT.ap.rearrange("p (k t) -> p k t", k=8)
    PH = {}

    def alloc_AD():
        PH["xs"] = [sbuf(F32, D) for _ in range(3)]
        PH["hsb"] = [sbuf(BF16, D) for _ in range(2)]
        PH["junk"] = sbuf(BF16, D)
        PH["wA"] = sbuf(BF16, 8 * 1536)
        PH["rope"] = sbuf(F32, 2 * TPC)
        PH["qkT"] = sbuf(BF16, 4 * TPC)

    def free_AD(keep=()):
        for k_ in ("xs", "hsb", "junk", "wA", "rope", "qkT"):
            if k_ in keep:
                continue
            v_ = PH.pop(k_)
            for b_ in (v_ if isinstance(v_, list) else [v_]):
                b_.free()

    alloc_AD()
    stat = [sbuf(F32, 16) for _ in range(4)]
    pT = [bank(BF16) for _ in range(2)]

    def nt_stage1(x_dram, n, cnt):
        xb = PH["xs"][cnt % 3]
        hb_ = PH["hsb"][cnt % 2]
        junk = PH["junk"]
        st = stat[cnt % 4]
        S.dma("sp", xb.v(), x_dram[n * 128:(n + 1) * 128, :], f"xs{cnt % 3}")
        act(junk.v(), xb.v(), AF.Square, accum=st.v(0, 1))
        act(st.v(2, 3), st.v(0, 1), AF.Ln, scale=1.0 / D, bias=1e-6)
        act(st.v(3, 4), st.v(2, 3), AF.Exp, scale=-0.5)
        ts("dve", hb_.v(), xb.v(), st.v(3, 4), None, ALU.mult)

    def nt_stage2(n, cnt):
        hb_ = PH["hsb"][cnt % 2]
        pt = pT[cnt % len(pT)]
        for k in range(8):
            tr(pt.v(k * 128, (k + 1) * 128), hb_.v(k * 128, (k + 1) * 128), ident.v())
        dstv = hT.cv(hT3[:, :, n * 128:(n + 1) * 128],
                     [(k * TPC + n * 128, k * TPC + (n + 1) * 128) for k in range(8)])
        srcv = pt.cv(pt.ap.rearrange("p (k t) -> p k t", k=8))
        cp("act" if cnt % 2 else "dve", dstv, srcv)

    def norm_transpose_all(x_dram, cnt0, hook=None):
        nt_stage1(x_dram, 0, cnt0)
        for n in range(NT):
            if n + 1 < NT:
                nt_stage1(x_dram, n + 1, cnt0 + n + 1)
            if hook is not None:
                hook(n)
            nt_stage2(n, cnt0 + n)
        return cnt0 + NT

    def hTv(kc, a, b):
        return hT.v(kc * TPC + a, kc * TPC + b)

    def wAv(kc, a, b):
        return PH["wA"].v(kc * 1536 + a, kc * 1536 + b)

    def cast_group_qk(wbuf, wv, kc, st):
        wscale(wv(kc, 0, 512), st.v(0, 512), kc)
        wscale(wv(kc, 1024, 1536), st.v(512, 1024), kc)
        s4 = st.ap[:, 0:512].rearrange("p (g h d) -> p g h d", g=8, h=2)
        d4 = wbuf.ap[:, kc * 1536 + 512:kc * 1536 + 1024].rearrange("p (g h d) -> p g h d", g=8, h=2)
        for hh in range(2):
            wscale(wbuf.cv(d4[:, :, hh, :], [(kc * 1536 + 512, kc * 1536 + 1024)]),
                   st.cv(s4[:, :, 1 - hh, :], [(0, 512)]), kc)

    for kc in range(8):
        st = load_w_cast(None, 0, w_in[kc * 128:(kc + 1) * 128, 2560:3584], 1024, kc)
        cast_group_qk(PH['wA'], wAv, kc, st)

    S.dma("sp", PH["rope"].v(), ropek_d, "rope")

    wB = sbuf(BF16, 8 * 2048)

    def wBv(kc, a, b):
        return wB.v(kc * 2048 + a, kc * 2048 + b)

    def wB_hook(n):
        kc, half = n // 2, n % 2
        st = load_w_cast(None, 0, w_in[kc * 128:(kc + 1) * 128, half * 1024:(half + 1) * 1024], 1024, kc)
        wscale(wBv(kc, half * 1024, (half + 1) * 1024), st.v(), kc)

    cnt = norm_transpose_all(xc, 0, hook=wB_hook)

    if "hT" in dbg:
        dump("hT", hT, BF16, 8 * TPC)

    pj = [bank() for _ in range(4)]
    pj_i = [0]

    def next_pj():
        b_ = pj[pj_i[0] % len(pj)]
        pj_i[0] += 1
        return b_

    tmp1 = [sbuf(F32, 512) for _ in range(2)]
    tmp2 = [sbuf(F32, 512) for _ in range(2)]

    def proj_fm(ps, wv, kc_cols, tg):
        a, b = kc_cols
        for kc in range(8):
            mm(ps.v(0, 512), wv(kc, a, b), hTv(kc, tg * 512, (tg + 1) * 512), start=(kc == 0), stop=(kc == 7))

    def rope_proj(wv, dst, i):
        rope = PH["rope"]
        for h in range(4):
            for tg in range(4):
                p1, p2 = next_pj(), next_pj()
                proj_fm(p1, wv, (h * 128, (h + 1) * 128), tg)
                proj_fm(p2, wv, (512 + h * 128, 512 + (h + 1) * 128), tg)
                t1, t2 = tmp1[i[0] % 2], tmp2[i[0] % 2]
                i[0] += 1
                tt("dve", t1.v(), p1.v(0, 512), rope.v(tg * 512, (tg + 1) * 512), ALU.mult)
                tt("dve", t2.v(), p2.v(0, 512), rope.v(TPC + tg * 512, TPC + (tg + 1) * 512), ALU.mult)
                tt("dve", dst.v(h * TPC + tg * 512, h * TPC + (tg + 1) * 512), t1.v(), t2.v(), ALU.add)

    ri = [0]
    qkT = PH['qkT']
    rope_proj(wAv, qkT, ri)
    for h in range(4):
        S.dma("pool", kb[h].v(), qkT.v(h * TPC, (h + 1) * TPC), "kb")
    vfull = sbuf(BF16, NT * 512)
    vst = []
    hv = sbuf(BF16, NT * 512)
    for n in range(NT):
        ps, ps2 = next_pj(), next_pj()
        for kc in range(8):
            mm(ps.v(0, 512), hTv(kc, n * 128, (n + 1) * 128), wAv(kc, 1024, 1536), start=(kc == 0), stop=(kc == 7))
            mm(ps2.v(0, 512), hTv(kc, n * 128, (n + 1) * 128), wBv(kc, 1024, 1536), start=(kc == 0), stop=(kc == 7))
        cp("act", vfull.v(n * 512, (n + 1) * 512), ps.v(0, 512))
        cp("dve", hv.v(n * 512, (n + 1) * 512), ps2.v(0, 512))
        if n % 8 == 7:
            s_ = n // 8
            vf4 = vfull.ap.rearrange("p (n h e) -> p n h e", n=NT, h=4)
            for h in range(4):
                dst = vb[h].ap[s_ * 128:(s_ + 1) * 128, :].rearrange("p (n e) -> p n e", n=8)
                S.dma("sp", vb[h].v(dst),
                      vfull.cv(vf4[:, s_ * 8:(s_ + 1) * 8, h, :], [(s_ * 8 * 512, (s_ + 1) * 8 * 512)]), "vb")
    def gather_kv0():
        allgather2(kb[0], kg1[0], kg[0], "K0")
        allgather2(vb[0], vg1[0], vg[0], "V0")

    gather_kv0()

    def gather_kv_rest(after=()):
        for h in range(1, 4):
            if h == 1 and after:
                S.add("pool", lambda e: e.nop(), list(after), [])
            allgather2(kb[h], kg1[h], kg[h], f"K{h}")
            allgather2(vb[h], vg1[h], vg[h], f"V{h}")

    if "kT" in dbg:
        dump("kT", qkT, BF16, 4 * TPC)

    vfull.free()
    free_AD()
    stop_after = [x for x in dbg if x.startswith("stop")]
    stop_after = stop_after[0] if stop_after else None

    hoT = sbuf(BF16, 4 * TPC)
    if stop_after != "stopA":
        HT = 1024
        for b_ in pj[2:] + pT[1:]:
            b_.free()
        del pj[2:]
        del pT[1:]
        pA = Buf(PS, F32, 1024)
        pO = Buf(PS, F32, 1024)
        pU = bank()
        assert pA.off % 2048 == 0 and pO.off % 2048 == 0

        def two(bk, a, b):
            return bk.v(a, b)

        per_head = {}

        def hgrn_local(h):
            oloc = sbuf(F32, TPC)
            sg = sbuf(BF16, TPC)
            qS = sbuf(BF16, TPC)
            Sst = [sbuf(F32, 128) for _ in range(2)]
            Sbf = [sbuf(BF16, 128) for _ in range(2)]
            ctot = sbuf(F32, 16)
            S.add("dve", lambda e: e.memset(Sst[0].ap, 0.0), [], [Sst[0].v()])
            si = 0
            for u in range(2):
                c0 = u * HT
                W1, W2, W3, W4 = sbuf(F32, HT), sbuf(F32, HT), sbuf(F32, HT), sbuf(F32, HT)
                W5, W6 = sbuf(F32, HT), sbuf(F32, HT)
                qe, ke, kd, kdT, AmT = (sbuf(BF16, HT) for _ in range(5))
                bl = sbuf(F32, 16)
                cinc = sbuf(F32, 16)
                cexc = sbuf(F32, 16)
                for t2 in range(2):
                    tg = u * 2 + t2
                    ps = next_pj()
                    proj_fm(ps, wBv, (512 + h * 128, 512 + (h + 1) * 128), tg)
                    act(W1.v(t2 * 512, (t2 + 1) * 512), ps.v(0, 512), AF.Exp, scale=-1.0)
                act(W1.v(), W1.v(), AF.Ln, bias=1.0)
                act(W1.v(), W1.v(), AF.Exp, scale=-1.0)
                ts("dve", W1.v(), W1.v(), oml.v(h, h + 1), lb.v(h, h + 1), ALU.mult, ALU.add)
                act(W2.v(), W1.v(), AF.Ln)
                S.add("dve", lambda e, W3=W3, W2=W2: e.tensor_tensor_scan(W3.ap, rmask.ap, W2.ap, 0.0, ALU.mult, ALU.add),
                      [rmask.v(), W2.v()], [W3.v()])
                act(W4.v(), W1.v(), AF.Identity, scale=-1.0, bias=1.0)
                act(W5.v(), W3.v(), AF.Exp)
                act(W6.v(), W3.v(), AF.Exp, scale=-1.0)
                for t2 in range(2):
                    tg = u * 2 + t2
                    ps = next_pj()
                    proj_fm(ps, wBv, (h * 128, (h + 1) * 128), tg)
                    tt("dve", W2.v(t2 * 512, (t2 + 1) * 512), ps.v(0, 512), W5.v(t2 * 512, (t2 + 1) * 512), ALU.mult)
                Wg = sbuf(F32, HT)
                for t2 in range(2):
                    tg = u * 2 + t2
                    ps = next_pj()
                    proj_fm(ps, wBv, (1536 + h * 128, 1536 + (h + 1) * 128), tg)
                    act(Wg.v(t2 * 512, (t2 + 1) * 512), ps.v(0, 512), AF.Exp, scale=-1.0)
                    act(Wg.v(t2 * 512, (t2 + 1) * 512), Wg.v(t2 * 512, (t2 + 1) * 512), AF.Ln, bias=1.0)
                    act(Wg.v(t2 * 512, (t2 + 1) * 512), Wg.v(t2 * 512, (t2 + 1) * 512), AF.Exp, scale=-1.0)
                    tt("dve", sg.v(tg * 512, (tg + 1) * 512), ps.v(0, 512), Wg.v(t2 * 512, (t2 + 1) * 512), ALU.mult)
                Wg.free()
                cp("act", qe.v(), W2.v())
                tt("dve", W4.v(), W4.v(), W6.v(), ALU.mult)
                cp("act", ke.v(), W4.v())
                eb3 = W5.ap.rearrange("p (c t) -> p c t", t=64)
                tt("dve", kd.cv(kd.ap.rearrange("p (c t) -> p c t", t=64)),
                   W4.cv(W4.ap.rearrange("p (c t) -> p c t", t=64)),
                   W5.cv(eb3[:, :, 63:64].to_broadcast([128, 16, 64])), ALU.mult)
                b3 = W3.ap.rearrange("p (c t) -> p c t", t=64)
                cp("dve", bl.cv(bl.ap[:, 0:16].rearrange("p (c o) -> p c o", o=1)), W3.cv(b3[:, :, 63:64]))
                init = 0.0 if u == 0 else ctot.ap[:, 0:1]
                rd = [ones_f.v(0, 16), bl.v()] + ([] if u == 0 else [ctot.v(0, 1)])
                S.add("dve", lambda e, cinc=cinc, bl=bl, init=init: e.tensor_tensor_scan(
                    cinc.ap[:, 0:16], ones_f.ap[:, 0:16], bl.ap[:, 0:16], init, ALU.mult, ALU.add), rd, [cinc.v()])
                tt("dve", cexc.v(), cinc.v(), bl.v(), ALU.subtract)
                act(cexc.v(), cexc.v(), AF.Exp)
                cp("dve", ctot.v(0, 1), cinc.v(15, 16))
                tt("dve", qS.cv(qS.ap[:, c0:c0 + HT].rearrange("p (c t) -> p c t", t=64), [(c0, c0 + HT)]),
                   W2.cv(W2.ap.rearrange("p (c t) -> p c t", t=64)),
                   cexc.cv(cexc.ap[:, 0:16].rearrange("p (c o) -> p c o", o=1).to_broadcast([128, 16, 64])), ALU.mult)
                pt = pT[0]
                for n in range(8):
                    tr(pt.v(n * 128, (n + 1) * 128), kd.v(n * 128, (n + 1) * 128), ident.v())
                cp("act", kdT.v(), pt.v())
                for n in range(8):
                    mm(two(pA, n * 128, (n + 1) * 128), ke.v(n * 128, (n + 1) * 128), qe.v(n * 128, (n + 1) * 128))
                tt("dve", AmT.cv(AmT.ap.rearrange("p (n t) -> p n t", t=128)),
                   View(two(pA, 0, 1024).ap.rearrange("p (n t) -> p n t", t=128), "ps", two(pA, 0, 1024).regs),
                   hmask.cv(hmask.ap.rearrange("p (o t) -> p o t", o=1).to_broadcast([128, 8, 128])), ALU.mult)
                def hvv(m_, p0, p1):
                    gn_ = u * 8 + m_ // 2
                    return hv.v(gn_ * 512 + h * 128, gn_ * 512 + (h + 1) * 128, p0, p1)

                def uview(m_):
                    return (pU.v(0, 128), pA.v(0, 128), pA.v(512, 640))[m_ % 3]

                def emit_U(m_):
                    n_, r0 = m_ // 2, (m_ % 2) * 64
                    mm(uview(m_), kdT.v(n_ * 128, (n_ + 1) * 128, r0, r0 + 64), hvv(m_, r0, r0 + 64))

                emit_U(0)
                emit_U(1)
                for m in range(16):
                    n, cc = m // 2, m % 2
                    gm = u * 16 + m
                    col = n * 128 + cc * 64
                    if cc == 0:
                        mm(two(pO, n * 128, (n + 1) * 128), hvv(m, 0, 128), AmT.v(n * 128, (n + 1) * 128), start=True, stop=False)
                    if m + 2 < 16:
                        emit_U(m + 2)
                    if gm > 0:
                        mm(two(pO, col, col + 64), Sbf[si % 2].v(), qe.v(col, col + 64), start=False, stop=True)
                    so, sn = Sst[si % 2], Sst[(si + 1) % 2]
                    stt("dve", sn.v(), so.v(), W5.v(m * 64 + 63, m * 64 + 64), uview(m), ALU.mult, ALU.add)
                    si += 1
                    cp("act", Sbf[si % 2].v(), sn.v())
                cp("act", oloc.v(c0, c0 + HT), two(pO, 0, 1024))
                for b_ in (W1, W2, W3, W4, W5, W6, qe, ke, kd, kdT, AmT, bl, cinc, cexc):
                    b_.free()
            sfin = Sst[si % 2]
            aseg = sbuf(F32, 144)
            act(aseg.v(128, 129), ctot.v(0, 1), AF.Exp)
            cp("dve", aseg.v(0, 128), sfin.v())
            S.dma("pool", stb[h].v(), aseg.v(0, 129), f"stb{h}")
            allgather2(stb[h], stg1[h], stg[h], f"S{h}")
            for b_ in Sst + Sbf + [ctot, aseg]:
                b_.free()
            per_head[h] = (oloc, sg, qS)

        def hgrn_fix(h):
            oloc, sg, qS = per_head[h]
            run = sbuf(F32, 128)
            mine = sbuf(F32, 128)
            mine_bf = sbuf(BF16, 128)
            gl = [sbuf(F32, 132) for _ in range(2)]
            S.add("dve", lambda e: e.memset(run.ap, 0.0), [], [run.v()])
            S.add("dve", lambda e: e.memset(mine.ap, 0.0), [], [mine.v()])
            for r in range(NCORES - 1):
                g_ = gl[r % 2]
                S.dma("sp", g_.v(0, 129), stg[h].v(stg[h].ap[r * 128:(r + 1) * 128, :]), f"sgl{r % 2}")
                stt("dve", run.v(), run.v(), g_.v(128, 129), g_.v(0, 128), ALU.mult, ALU.add)
                stt("dve", mine.v(), run.v(), sel.v(r + 1, r + 2), mine.v(), ALU.mult, ALU.add)
            cp("act", mine_bf.v(), mine.v())
            for tg in range(4):
                a, b = tg * 512, (tg + 1) * 512
                ps = next_pj()
                mm(ps.v(0, 512), mine_bf.v(), qS.v(a, b))
                tt("dve", oloc.v(a, b), oloc.v(a, b), ps.v(0, 512), ALU.add)
                sq = tmp1[tg % 2]
                act(sq.v(), oloc.v(a, b), AF.Square)
                ps2 = next_pj()
                mm(ps2.v(0, 512), ones_f.v(), sq.v())
                t2_ = tmp2[tg % 2]
                act(t2_.v(), ps2.v(0, 512), AF.Ln, scale=1.0 / 128, bias=1e-6)
                act(t2_.v(), t2_.v(), AF.Exp, scale=-0.5)
                tt("dve", t2_.v(), oloc.v(a, b), t2_.v(), ALU.mult)
                stt("dve", hoT.v(h * TPC + a, h * TPC + b), t2_.v(), hnw.v(0, 1), sg.v(a, b), ALU.mult, ALU.mult)
            S.dma("pool", hob[h].v(), hoT.v(h * TPC, (h + 1) * TPC), "hob")
            for b_ in (oloc, sg, qS, run, mine, mine_bf) + tuple(gl):
                b_.free()

        hgrn_local(0)
        hgrn_local(1)
        hgrn_fix(0)
        hgrn_local(2)
        hgrn_fix(1)
        hgrn_local(3)
        for b_ in [pA, pO, pU]:
            b_.free()
        pj.extend([bank(), bank()])
        pT.append(bank(BF16))
        hv.free()
        wB.free()

        if "hoT" in dbg:
            dump("hoT", hoT, BF16, 4 * TPC)

    if stop_after not in ("stopA", "stopC"):
        alloc_AD()
        qkT = PH["qkT"]
        for kc in range(8):
            st = load_w_cast(None, 0, [w_in[kc * 128:(kc + 1) * 128, 2048:2560],
                                       w_in[kc * 128:(kc + 1) * 128, 3584:4096]], 1024, kc)
            cast_group_qk(PH['wA'], wAv, kc, st)
        S.dma("sp", PH["rope"].v(), ropeq_d, "rope")
        cnt = norm_transpose_all(xi, cnt)
        rope_proj(wAv, qkT, ri)
        if stop_after != "stopA":
            hgrn_fix(2)
            hgrn_fix(3)
        sgd = sbuf(BF16, 4 * TPC)
        for h in range(4):
            for tg in range(4):
                ps = next_pj()
                proj_fm(ps, wAv, (1024 + h * 128, 1024 + (h + 1) * 128), tg)
                t1 = tmp1[tg % 2]
                act(t1.v(), ps.v(0, 512), AF.Exp, scale=-1.0)
                act(t1.v(), t1.v(), AF.Ln, bias=1.0)
                act(t1.v(), t1.v(), AF.Exp, scale=-1.0)
                tt("dve", sgd.v(h * TPC + tg * 512, h * TPC + (tg + 1) * 512), ps.v(0, 512), t1.v(), ALU.mult)
        if "qT" in dbg:
            dump("qT", qkT, BF16, 4 * TPC)
        gather_kv_rest(after=[b_.v() for b_ in PH["xs"]] + [PH["wA"].v()])
        for h in range(4):
            allgather2(hob[h], hog1[h], hog[h], f"H{h}")
        xs_keep = PH["xs"][0]
        for b_ in PH["xs"][1:]:
            b_.free()
        free_AD(keep=("xs", "qkT"))
        for b_ in [hT] + pT + pj + wst[1:]:
            b_.free()
        wst_keep = wst[0]

    if stop_after not in ("stopA", "stopC", "stopD"):
        for b_ in tmp1 + tmp2:
            b_.free()
        od = sbuf(F32, 4 * TPC)
        mixd = sbuf(BF16, 4 * TPC)
        RK = 4
        kring = [[sbuf(BF16, 1024) for _ in range(RK)] for _ in range(2)]
        for c_ in range(2):
            for b_ in kring[c_]:
                S.add("dve", lambda e, b_=b_: e.memset(b_.ap, 0.0), [], [b_.v()])
        vring = [sbuf(BF16, 1024) for _ in range(RK)]
        RE = 5
        ering = [sbuf(BF16, 1024) for _ in range(RE)]
        pS2 = [Buf(PS, F32, 1024), Buf(PS, F32, 1024)]
        for pr in pS2:
            assert pr.off % 2048 == 0
        pacc = [bank(), bank()]
        pden = [bank(), bank()]
        ft = [sbuf(F32, 512) for _ in range(4)]
        esum = [sbuf(F32, 1024) for _ in range(2)]
        etmp = [sbuf(BF16, 1024) for _ in range(2)]

        sbs = []
        for h in range(4):
            for g in range(4):
                for sb_ in range(4 * g + 4):
                    sbs.append((h, g, sb_))
        tiles = []
        for i, (h, g, sb_) in enumerate(sbs):
            for kt in range(8):
                tiles.append((i, kt))

        def load_sb(i):
            h, g, sb_ = sbs[i]
            rank, sl = sb_ // 2, sb_ % 2
            row = rank * 128
            for c_ in range(2):
                S.dma("sp", kring[c_][i % RK].v(0, 1024, c_ * 64, (c_ + 1) * 64),
                      kg[h].v(kg[h].ap[row + c_ * 64:row + (c_ + 1) * 64, sl * 1024:(sl + 1) * 1024]), f"kr{c_}{i % RK}")
            vrow = (rank * 2 + sl) * 128
            S.dma("sp", vring[i % RK].v(), vg[h].v(vg[h].ap[vrow:vrow + 128, :]), f"vr{i % RK}")

        def tile_info(ti):
            i, kt = tiles[ti]
            h, g, sb_ = sbs[i]
            r = sb_ - 4 * g
            c0 = 128 * r if r > 0 else 0
            return i, kt, h, g, sb_, r, c0

        def qk(ti):
            i, kt, h, g, sb_, r, c0 = tile_info(ti)
            pr2 = pS2[ti % 2]
            q0 = h * TPC + g * 512
            for c in range(2):
                ks = kring[c][i % RK]
                first = True
                if r >= 0:
                    mm(pr2.v(c * 512 + 128 * r, c * 512 + 128 * r + 128), ident.v(), amask.v(kt * 128, (kt + 1) * 128), start=True, stop=False)
                    first = False
                mm(pr2.v(c * 512 + c0, c * 512 + 512), ks.v(kt * 128, (kt + 1) * 128), qkT.v(q0 + c0, q0 + 512),
                   start=first, stop=True)

        def ex(ti):
            i, kt, h, g, sb_, r, c0 = tile_info(ti)
            pr2 = pS2[ti % 2]
            es = ering[ti % RE]
            src = pr2.cv(pr2.ap.rearrange("p (c q) -> p c q", c=2)[:, :, c0:512], [(c0, 512), (512 + c0, 1024)])
            dst = es.cv(es.ap.rearrange("p (c q) -> p c q", c=2)[:, :, c0:512], [(c0, 512), (512 + c0, 1024)])
            act(dst, src, AF.Exp, scale=0.125)

        def pv(ti):
            i, kt, h, g, sb_, r, c0 = tile_info(ti)
            es = ering[ti % RE]
            vs_ = vring[i % RK]
            first = (sb_ == 0 and kt == 0)
            last = (sb_ == 4 * g + 3 and kt == 7)
            gi = h * 4 + g
            eb_ = esum[gi % 2]
            for c in range(2):
                mm(pacc[c].v(c0, 512), vs_.v(kt * 128, (kt + 1) * 128), es.v(c * 512 + c0, c * 512 + 512), start=first, stop=last)
            rg_ = [(c0, 512), (512 + c0, 1024)]
            e3 = es.cv(es.ap.rearrange("p (c q) -> p c q", c=2)[:, :, c0:512], rg_)
            tb_ = etmp[i % 2]
            t3 = tb_.cv(tb_.ap.rearrange("p (c q) -> p c q", c=2)[:, :, c0:512], rg_)
            s3 = eb_.cv(eb_.ap.rearrange("p (c q) -> p c q", c=2)[:, :, c0:512], rg_)
            if kt == 0:
                cp("dve", t3, e3)
            else:
                tt("dve", t3, e3, t3, ALU.add)
            if kt == 7:
                if sb_ == 0:
                    cp("dve", s3, t3)
                else:
                    tt("dve", s3, t3, s3, ALU.add)
            if last:
                for c in range(2):
                    mm(pden[c].v(0, 512), ones_f.v(), eb_.v(c * 512, (c + 1) * 512))
                    act(ft[c].v(), pden[c].v(0, 512), AF.Ln)
                    act(ft[c].v(), ft[c].v(), AF.Exp, scale=-1.0)
                tt("dve", ft[2].v(), pacc[0].v(0, 512), ft[0].v(), ALU.mult)
                tt("dve", ft[3].v(), pacc[1].v(0, 512), ft[1].v(), ALU.mult)
                o0 = h * TPC + g * 512
                stt("dve", od.v(o0, o0 + 512), ft[3].v(), neglam.v(0, 1), ft[2].v(), ALU.mult, ALU.add)

        FP = {}

        def phaseF_prep():
            woutbf = sbuf(BF16, 8 * D)
            for kc in range(8):
                S.dma("sp", wst_keep.v(), w_out[kc * 128:(kc + 1) * 128, :], "wst0")
                cp("act" if kc % 2 else "dve", woutbf.v(kc * D, (kc + 1) * D), wst_keep.v())
                yield
            hoR = [sbuf(BF16, NT * 128) for _ in range(4)]
            for h in range(4):
                hoR4 = hoR[h].ap.rearrange("p (a s t) -> p a s t", a=NCORES, s=2)
                for s_ in range(2):
                    def _ld(e, h=h, s_=s_, hoR4=hoR4):
                        pid = 0 if "nodyn" in dbg else PID["sp"]
                        src = hog[h].ap[:, bass.ds(pid * 128 + s_ * 1024, 128)]
                        return e.dma_start(out=hoR4[:, :, s_, :], in_=src.rearrange("(a e) t -> e a t", a=NCORES))
                    S.add("sp", _ld, [hog[h].v()], [hoR[h].v()], kind="d", group="hoR")
                yield
            FP.update(woutbf=woutbf, hoR=hoR)

        FPG = [phaseF_prep()]
        LA = 2
        for i in range(min(LA, len(sbs))):
            load_sb(i)
        qk(0)
        for ti in range(len(tiles)):
            i, kt = tiles[ti]
            if kt == 0 and i + LA < len(sbs):
                load_sb(i + LA)
            if kt == 0 and sbs[i][0] >= 2 and FPG:
                try:
                    next(FPG[0])
                except StopIteration:
                    FPG.clear()
            if ti + 1 < len(tiles):
                qk(ti + 1)
            ex(ti)
            pv(ti)
        for b_ in kring[0] + kring[1] + vring + ering + pS2 + pacc + pden + ft + esum + etmp:
            b_.free()
        if "od" in dbg:
            dump("od", od, F32, 4 * TPC)

        pn = [bank() for _ in range(3)]
        nt_ = [sbuf(F32, 512) for _ in range(4)]
        grp = [(h, g) for h in range(4) for g in range(4)]

        def pp_a(k_):
            h, g = grp[k_]
            a, b = h * TPC + g * 512, h * TPC + (g + 1) * 512
            sq, ps = nt_[(k_ % 2) * 2], pn[k_ % 3]
            act(sq.v(), od.v(a, b), AF.Square)
            mm(ps.v(0, 512), ones_f.v(), sq.v())

        def pp_b(k_):
            h, g = grp[k_]
            a, b = h * TPC + g * 512, h * TPC + (g + 1) * 512
            t2_, ps = nt_[(k_ % 2) * 2 + 1], pn[k_ % 3]
            act(t2_.v(), ps.v(0, 512), AF.Ln, scale=1.0 / 128, bias=1e-5)
            act(t2_.v(), t2_.v(), AF.Exp, scale=-0.5)
            tt("dve", t2_.v(), od.v(a, b), t2_.v(), ALU.mult)
            stt("dve", mixd.v(a, b), t2_.v(), dnw.v(0, 1), sgd.v(a, b), ALU.mult, ALU.mult)

        pp_a(0)
        for k_ in range(16):
            if k_ + 1 < 16:
                pp_a(k_ + 1)
            pp_b(k_)
        for b_ in pn + nt_:
            b_.free()
        od.free()
        sgd.free()
        qkT.free()
        if "mixd" in dbg:
            dump("mixd", mixd, BF16, 4 * TPC)

        woutbf, hoR = FP["woutbf"], FP["hoR"]
        po = [bank() for _ in range(8)]
        yb = [sbuf(F32, D) for _ in range(2)]
        ob = [sbuf(F32, D) for _ in range(2)]
        xr = [xs_keep, sbuf(F32, D)]
        fst = [sbuf(F32, 16) for _ in range(2)]
        junk2 = sbuf(BF16, D)
        def pf_a(jj):
            xb = xr[jj % 2]
            S.dma("sp", xb.v(), xi[jj * 128:(jj + 1) * 128, :], f"xr{jj % 2}")
            for half in range(2):
                ps = po[(jj % 4) * 2 + half]
                for kc in range(8):
                    if kc < 4:
                        l = hoR[kc].v(jj * 128, (jj + 1) * 128)
                    else:
                        l = mixd.v((kc - 4) * TPC + jj * 128, (kc - 4) * TPC + (jj + 1) * 128)
                    mm(ps.v(0, 512), l, woutbf.v(kc * D + half * 512, kc * D + (half + 1) * 512),
                       start=(kc == 0), stop=(kc == 7))
                tt("dve", yb[jj % 2].v(half * 512, (half + 1) * 512), xb.v(half * 512, (half + 1) * 512), ps.v(0, 512), ALU.add)
            act(junk2.v(), yb[jj % 2].v(), AF.Square, accum=fst[jj % 2].v(0, 1))

        def pf_b(jj):
            st = fst[jj % 2]
            act(st.v(2, 3), st.v(0, 1), AF.Ln, scale=1.0 / D, bias=1e-6)
            act(st.v(3, 4), st.v(2, 3), AF.Exp, scale=-0.5)
            stt("dve", ob[jj % 2].v(), yb[jj % 2].v(), st.v(3, 4), fnw.v(), ALU.mult, ALU.mult)
            S.dma("sp", y_d[jj * 128:(jj + 1) * 128, :], ob[jj % 2].v(), "out")

        pf_a(0)
        for jj in range(NT):
            if jj + 1 < NT:
                pf_a(jj + 1)
            pf_b(jj)

    S.analyze()
    esem = {e: nc.alloc_semaphore(f"s_{e}") for e in S.ENGS}
    gsem = {g: nc.alloc_semaphore(f"g_{i}") for i, g in enumerate(S.groups)}
    with nc.Block() as block:
        @block.tensor
        def _(h):
            S.emit_engine("pe", h, esem, gsem)

        @block.scalar
        def _(h):
            S.emit_engine("act", h, esem, gsem)

        @block.vector
        def _(h):
            S.emit_engine("dve", h, esem, gsem)

        @block.gpsimd
        def _(h):
            S.emit_engine("pool", h, esem, gsem)

        @block.sync
        def _(h):
            PID["sp"] = h.partition_id()
            S.emit_engine("sp", h, esem, gsem)
            for g, arr in S.gpos.items():
                h.wait_ge(gsem[g], arr[-1][1])
    return nc, S, (SB.peak, PS.peak)


def _rope_table(pos):
    half = 32
    inv_freq = (1.0 / (np.float32(10000.0) ** (np.arange(half, dtype=np.float32) / np.float32(half)))).astype(np.float32)
    ang = pos.astype(np.float32)[None, :] * inv_freq[:, None]
    c, s = np.cos(ang).astype(np.float32), np.sin(ang).astype(np.float32)
    ct = np.concatenate([c, c, c, c], 0)
    stb = np.concatenate([-s, s, -s, s], 0)
    return np.ascontiguousarray(np.concatenate([ct, stb], 1))


def make_in_maps(x, norm_w, w_in, hgrn_lb_logits, hgrn_norm_w, diff_lambda_q1, diff_lambda_k1,
                 diff_lambda_q2, diff_lambda_k2, diff_norm_w, w_out, final_norm_w):
    f32 = np.float32
    x2 = np.asarray(x, f32).reshape(T, D)
    w_in2 = np.ascontiguousarray(np.asarray(w_in, f32).reshape(D, 4096))
    w_out2 = np.ascontiguousarray(np.asarray(w_out, f32).reshape(D, D))
    nw = np.ascontiguousarray(np.asarray(norm_w, f32).reshape(8, 128).T)
    lbl = np.ascontiguousarray(np.asarray(hgrn_lb_logits, f32).reshape(2, 4, 128).transpose(2, 0, 1).reshape(128, 8))
    hnw = np.ascontiguousarray(np.asarray(hgrn_norm_w, f32).reshape(128, 1))
    dnw = np.ascontiguousarray(np.asarray(diff_norm_w, f32).reshape(128, 1))
    lamv = np.concatenate([np.asarray(a, f32).reshape(64) for a in
                           (diff_lambda_q1, diff_lambda_k1, diff_lambda_q2, diff_lambda_k2)])
    lamv = np.ascontiguousarray(np.broadcast_to(lamv[None, :], (128, 256)))
    fnw = np.ascontiguousarray(np.broadcast_to(np.asarray(final_norm_w, f32).reshape(1, D), (128, D)))
    ident = np.eye(128, dtype=f32).astype(ml_dtypes.bfloat16)
    p = np.arange(128)
    hmask = ((p[:, None] // 64 == p[None, :] // 64) & (p[:, None] <= p[None, :])).astype(f32)
    xt = x2.reshape(128, 128, D)
    maps = []
    for c in range(NCORES):
        xc = np.ascontiguousarray(x2[c * TPC:(c + 1) * TPC])
        tiles = c + 8 * np.arange(NT)
        xi = np.ascontiguousarray(xt[tiles].reshape(TPC, D))
        posk = np.arange(c * TPC, (c + 1) * TPC)
        posq = (tiles[:, None] * 128 + np.arange(128)[None, :]).reshape(-1)
        am = np.zeros((128, 8, 128), f32)
        for m in range(8):
            if m > c:
                am[:, m, :] = NEG
            elif m == c:
                am[:, m, :] = np.where(p[:, None] <= p[None, :], 0.0, NEG)
        sel = np.zeros((128, 8), f32)
        sel[:, c] = 1.0
        maps.append({
            "xc": xc, "xi": xi, "w_in": w_in2, "w_out": w_out2, "nw": nw, "lbl": lbl, "hnw": hnw,
            "dnw": dnw, "lamv": lamv, "fnw": fnw, "ropek": _rope_table(posk), "ropeq": _rope_table(posq),
            "amask": np.ascontiguousarray(am.reshape(128, 1024)).astype(ml_dtypes.bfloat16),
            "hmask": hmask, "ident": ident, "sel": sel,
        })
    return maps


_PROG = {}


def kernel(**inputs):
    if "p" not in _PROG:
        _PROG["p"] = build_program()
    nc = _PROG["p"][0]
    maps = make_in_maps(**inputs)
    res = run_bass_kernel_spmd(nc, maps, core_ids=list(range(NCORES)))
    out = np.empty((128, 128, D), np.float32)
    for c in range(NCORES):
        yc = np.asarray(res.results[c]["y"], np.float32).reshape(NT, 128, D)
        out[c + 8 * np.arange(NT)] = yc
    return out.reshape(1, T, D)
```
